# Optimizing a Trainium2 kernel written in Bass

```python
import jax
import jax.numpy as jnp
from jax import lax
import numpy as np


D_MODEL = 1024
BATCH = 8
SEQ = 4096
DEPTH = 2

GRID_W = 64
ROPE_THETA = 10000.0
NORM_EPS = 1e-6
NEG_INF = -1e30

MLA_HEADS = 4
MLA_NOPE = 128
MLA_ROPE = 64
MLA_V = 128
MLA_Q_RANK = 256
MLA_KV_RANK = 128
MLA_QBLOCK = 128
DIL_PATTERNS = ((128, 1), (512, 4), (2048, 16))
DIL_HEADS = 4
DIL_DH = 64
DIL_QBLOCK = 64
NAT_HEADS = 8
NAT_DH = 64
NAT_KH_MAX = 8
NAT_KW = 16
NAT_QCOL = 16
NAT_KCOL = NAT_QCOL + NAT_KW
GLA_HEADS = 4
GLA_DK = 64
GLA_DV = 128
GLA_GATE_RANK = 16
GLA_TAU = 16.0
GLA_CHUNK = 64

W_A = MLA_HEADS * MLA_V
W_B = DIL_HEADS * DIL_DH
W_C = NAT_HEADS * NAT_DH
W_D = GLA_HEADS * GLA_DV
N_BRANCH = 4

IN_SIZES = (MLA_Q_RANK, MLA_KV_RANK, MLA_ROPE,
            3 * len(DIL_PATTERNS) * W_B,
            W_C, W_C, W_C,
            GLA_HEADS * GLA_DK, GLA_HEADS * GLA_DK, W_D, GLA_GATE_RANK, GLA_GATE_RANK,
            W_A, W_B, W_C, W_D)
IN_SPLITS = tuple(int(s) for s in np.cumsum(IN_SIZES)[:-1])
D_IN = int(sum(IN_SIZES))

kernel_name = 'hybrid_gated_mla_dilated_nat_gla_encoder'


def rms_norm(x, g):
    xf = x.astype(jnp.float32)
    y = xf * lax.rsqrt(jnp.mean(xf * xf, axis=-1, keepdims=True) + NORM_EPS)
    return (y * g.astype(jnp.float32)).astype(x.dtype)


def rope(x, pos):
    dim = x.shape[-1]
    inv = jnp.power(ROPE_THETA, -jnp.arange(0, dim, 2, dtype=jnp.float32) / dim)
    ang = pos.astype(jnp.float32)[:, None] * inv[None, :]
    cos = jnp.cos(ang)[None, :, None, :]
    sin = jnp.sin(ang)[None, :, None, :]
    xf = x.astype(jnp.float32)
    x1, x2 = xf[..., :dim // 2], xf[..., dim // 2:]
    return jnp.concatenate([x1 * cos - x2 * sin, x2 * cos + x1 * sin], axis=-1).astype(x.dtype)


def mla_attention(q_lat, kv_lat, k_rope, q_norm_g, w_uq, kv_norm_g, w_ukv, pos):
    B_, S_, _ = q_lat.shape
    q = (rms_norm(q_lat, q_norm_g) @ w_uq).reshape(B_, S_, MLA_HEADS, MLA_NOPE + MLA_ROPE)
    q_nope, q_pe = q[..., :MLA_NOPE], rope(q[..., MLA_NOPE:], pos)
    kv = (rms_norm(kv_lat, kv_norm_g) @ w_ukv).reshape(B_, S_, MLA_HEADS, MLA_NOPE + MLA_V)
    k_nope, v = kv[..., :MLA_NOPE], kv[..., MLA_NOPE:]
    k_pe = rope(k_rope[:, :, None, :], pos)[:, :, 0]
    scale = (MLA_NOPE + MLA_ROPE) ** -0.5
    nqb = S_ // MLA_QBLOCK

    def blocks(t):
        return jnp.moveaxis(t.reshape(B_, nqb, MLA_QBLOCK, *t.shape[2:]), 1, 0)

    def attend(qb):
        qn, qr = qb
        s = jnp.einsum('bqhd,bkhd->bhqk', qn, k_nope) + jnp.einsum('bqhd,bkd->bhqk', qr, k_pe)
        p = jax.nn.softmax(s.astype(jnp.float32) * scale, axis=-1)
        return jnp.einsum('bhqk,bkhd->bqhd', p.astype(v.dtype), v)

    o = lax.map(attend, (blocks(q_nope), blocks(q_pe)))
    return jnp.moveaxis(o, 0, 1).reshape(B_, S_, W_A)


def dilated_group(q, k, v, window, dilation):
    B_, S_, H_, dh = q.shape
    r = dilation
    R = window // (2 * dilation)
    L = S_ // r
    QB = DIL_QBLOCK
    nb = -(-L // QB)
    Lp = nb * QB

    def to_sub(t):
        return t.reshape(B_, L, r, H_, dh).transpose(0, 2, 3, 1, 4)

    pad_q = ((0, 0), (0, 0), (0, 0), (0, Lp - L), (0, 0))
    pad_k = ((0, 0), (0, 0), (0, 0), (R, Lp - L + R), (0, 0))
    qs = jnp.pad(to_sub(q), pad_q)
    kp = jnp.pad(to_sub(k), pad_k)
    vp = jnp.pad(to_sub(v), pad_k)
    key_idx = np.arange(nb)[:, None] * QB + np.arange(QB + 2 * R)[None, :]
    kb = kp[:, :, :, key_idx]
    vb = vp[:, :, :, key_idx]
    qb = qs.reshape(B_, r, H_, nb, QB, dh)
    qpos = np.arange(nb)[:, None] * QB + np.arange(QB)[None, :]
    kpos = key_idx - R
    mask = ((np.abs(qpos[:, :, None] - kpos[:, None, :]) <= R)
            & (kpos[:, None, :] >= 0) & (kpos[:, None, :] < L))
    s = jnp.einsum('bmhnqd,bmhnkd->bmhnqk', qb, kb).astype(jnp.float32) * (dh ** -0.5)
    s = jnp.where(mask, s, NEG_INF)
    m = jnp.max(s, axis=-1, keepdims=True)
    p = jnp.exp(s - m)
    den = jnp.sum(p, axis=-1, keepdims=True)
    o = jnp.einsum('bmhnqk,bmhnkd->bmhnqd', p, vb.astype(jnp.float32)) / den
    lse = (m + jnp.log(den))[..., 0]
    o = o.reshape(B_, r, H_, Lp, dh)[:, :, :, :L].transpose(0, 3, 1, 2, 4).reshape(B_, S_, H_, dh)
    lse = lse.reshape(B_, r, H_, Lp)[..., :L].transpose(0, 3, 1, 2).reshape(B_, S_, H_)
    return o, lse


def dilated_attention(qkv, pos):
    B_, S_, _ = qkv.shape
    qkv = qkv.reshape(B_, S_, len(DIL_PATTERNS), 3, DIL_HEADS, DIL_DH)
    outs, lses = [], []
    for g, (window, dilation) in enumerate(DIL_PATTERNS):
        q = rope(qkv[:, :, g, 0], pos)
        k = rope(qkv[:, :, g, 1], pos)
        o, l = dilated_group(q, k, qkv[:, :, g, 2], window, dilation)
        outs.append(o)
        lses.append(l)
    w = jax.nn.softmax(jnp.stack(lses, axis=0), axis=0)
    o = jnp.sum(w[..., None] * jnp.stack(outs, axis=0), axis=0)
    return o.reshape(B_, S_, W_B).astype(qkv.dtype)


def neighbourhood_attention(q, k, v, rpb):
    B_, S_, _ = q.shape
    rows = S_ // GRID_W
    kh = min(NAT_KH_MAX, rows)

    def grid(t):
        return t.reshape(B_, rows, GRID_W, NAT_HEADS, NAT_DH).transpose(0, 3, 1, 2, 4)

    qg, kg, vg = grid(q), grid(k), grid(v)
    ncb = GRID_W // NAT_QCOL
    qcol = np.arange(GRID_W).reshape(ncb, NAT_QCOL)
    kcol_start = np.clip(np.arange(ncb) * NAT_QCOL - NAT_KW // 2, 0, GRID_W - NAT_KCOL)
    kcol = kcol_start[:, None] + np.arange(NAT_KCOL)[None, :]
    win_start = np.clip(qcol - NAT_KW // 2, 0, GRID_W - NAT_KW)
    col_mask = ((kcol[:, None, :] >= win_start[:, :, None])
                & (kcol[:, None, :] < win_start[:, :, None] + NAT_KW))
    col_off = np.clip(kcol[:, None, :] - qcol[:, :, None] + NAT_KW - 1, 0, 2 * NAT_KW - 2)
    rpb_cols = rpb[:, :, col_off]
    scale = NAT_DH ** -0.5

    def attend_row(r):
        rs = jnp.clip(r - kh // 2, 0, rows - kh)
        q_r = lax.dynamic_index_in_dim(qg, r, axis=2, keepdims=False)
        q_r = q_r.reshape(B_, NAT_HEADS, ncb, NAT_QCOL, NAT_DH)
        k_rows = lax.dynamic_slice_in_dim(kg, rs, kh, axis=2)[:, :, :, kcol]
        v_rows = lax.dynamic_slice_in_dim(vg, rs, kh, axis=2)[:, :, :, kcol]
        row_off = rs + jnp.arange(kh) - r + NAT_KH_MAX - 1
        bias = jnp.take(rpb_cols, row_off, axis=1).transpose(0, 2, 3, 1, 4)
        s = jnp.einsum('bhjqd,bhrjcd->bhjqrc', q_r, k_rows).astype(jnp.float32) * scale
        s = s + bias.astype(jnp.float32)[None]
        s = jnp.where(col_mask[None, None, :, :, None, :], s, NEG_INF)
        shp = s.shape
        p = jax.nn.softmax(s.reshape(*shp[:4], kh * NAT_KCOL), axis=-1).reshape(shp)
        o = jnp.einsum('bhjqrc,bhrjcd->bhjqd', p.astype(v_rows.dtype), v_rows)
        return o.reshape(B_, NAT_HEADS, GRID_W, NAT_DH)

    o = lax.map(attend_row, jnp.arange(rows))
    return o.transpose(1, 0, 3, 2, 4).reshape(B_, S_, W_C)


def gla_direction(q, k, v, log_a):
    B_, S_, H_, dk = q.shape
    dv = v.shape[-1]
    C = GLA_CHUNK
    n = S_ // C

    def ch(t):
        return t.reshape(B_, n, C, H_, t.shape[-1])

    q, k, v, log_a = ch(q), ch(k), ch(v), ch(log_a)
    b = jnp.cumsum(log_a, axis=2)
    b_last = b[:, :, -1]
    b_mid = b[:, :, C // 2 - 1][:, :, None]
    lower = np.tril(np.ones((C, C), dtype=bool))
    att = jnp.einsum('bnihd,bnjhd->bnhij', q * jnp.exp(b - b_mid), k * jnp.exp(b_mid - b))
    att = jnp.where(lower, att, 0.0)
    o = jnp.einsum('bnhij,bnjhe->bnihe', att, v)
    kv = jnp.einsum('bnjhd,bnjhe->bnhde', k * jnp.exp(b_last[:, :, None] - b), v)

    def step(state, inp):
        decay, kv_c = inp
        return decay[..., None] * state + kv_c, state

    _, states = lax.scan(step, jnp.zeros((B_, H_, dk, dv), jnp.float32),
                         (jnp.moveaxis(jnp.exp(b_last), 1, 0), jnp.moveaxis(kv, 1, 0)))
    o = o + jnp.einsum('bnihd,nbhde->bnihe', q * jnp.exp(b), states)
    return o.reshape(B_, S_, H_, dv)


def gla_attention(q, k, v, g_f, g_b, w_gf, b_gf, w_gb, b_gb, norm_g):
    B_, S_, _ = q.shape

    def heads(t, d):
        return t.astype(jnp.float32).reshape(B_, S_, GLA_HEADS, d)

    qh = heads(q, GLA_DK) * (GLA_DK ** -0.5)
    kh = heads(k, GLA_DK)
    vh = heads(v, GLA_DV)
    la_f = heads(jax.nn.log_sigmoid((g_f @ w_gf + b_gf).astype(jnp.float32)) / GLA_TAU, GLA_DK)
    la_b = heads(jax.nn.log_sigmoid((g_b @ w_gb + b_gb).astype(jnp.float32)) / GLA_TAU, GLA_DK)

    def flip(t):
        return jnp.flip(t, axis=1)

    o = gla_direction(qh, kh, vh, la_f) + flip(gla_direction(flip(qh), flip(kh), flip(vh), flip(la_b)))
    o = rms_norm(o, norm_g)
    return o.reshape(B_, S_, W_D).astype(q.dtype)


def hybrid_layer(x, pos, norm_g, w_in, q_norm_g, w_uq, kv_norm_g, w_ukv, rpb,
                 w_gf, b_gf, w_gb, b_gb, gla_norm_g, w_pa, w_pb, w_pc, w_pd,
                 w_merge, b_merge, w_out):
    B_, S_, _ = x.shape
    h = rms_norm(x, norm_g)
    (a_q, a_kv, a_kr, b_qkv, c_q, c_k, c_v, d_q, d_k, d_v, d_gf, d_gb,
     z_a, z_b, z_c, z_d) = jnp.split(h @ w_in, IN_SPLITS, axis=-1)
    o_a = mla_attention(a_q, a_kv, a_kr, q_norm_g, w_uq, kv_norm_g, w_ukv, pos) * jax.nn.silu(z_a)
    o_b = dilated_attention(b_qkv, pos) * jax.nn.silu(z_b)
    o_c = neighbourhood_attention(c_q, c_k, c_v, rpb) * jax.nn.silu(z_c)
    o_d = gla_attention(d_q, d_k, d_v, d_gf, d_gb, w_gf, b_gf, w_gb, b_gb, gla_norm_g) * jax.nn.silu(z_d)
    gates = jax.nn.sigmoid((h @ w_merge + b_merge).astype(jnp.float32)).astype(x.dtype)
    gates = gates.reshape(B_, S_, N_BRANCH, x.shape[-1])
    mixed = (gates[:, :, 0] * (o_a @ w_pa) + gates[:, :, 1] * (o_b @ w_pb)
             + gates[:, :, 2] * (o_c @ w_pc) + gates[:, :, 3] * (o_d @ w_pd))
    return x + mixed @ w_out


def setup_inputs(seed: int = 0) -> dict:
    key = jax.random.key(seed)
    ks = jax.random.split(key, 21)

    def nrm(k, shape, scale):
        return jax.random.normal(k, shape, jnp.float32) * scale

    def gain(k, shape):
        return 1.0 + 0.02 * jax.random.normal(k, shape, jnp.float32)

    L_, D = DEPTH, D_MODEL
    return {
        'x': nrm(ks[0], (BATCH, SEQ, D), 1.0),
        'norm_g': gain(ks[1], (L_, D)),
        'w_in': nrm(ks[2], (L_, D, D_IN), D ** -0.5),
        'mla_q_norm_g': gain(ks[3], (L_, MLA_Q_RANK)),
        'mla_w_uq': nrm(ks[4], (L_, MLA_Q_RANK, MLA_HEADS * (MLA_NOPE + MLA_ROPE)), MLA_Q_RANK ** -0.5),
        'mla_kv_norm_g': gain(ks[5], (L_, MLA_KV_RANK)),
        'mla_w_ukv': nrm(ks[6], (L_, MLA_KV_RANK, MLA_HEADS * (MLA_NOPE + MLA_V)), MLA_KV_RANK ** -0.5),
        'nat_rpb': nrm(ks[7], (L_, NAT_HEADS, 2 * NAT_KH_MAX - 1, 2 * NAT_KW - 1), 0.1),
        'gla_w_gate_f': nrm(ks[8], (L_, GLA_GATE_RANK, GLA_HEADS * GLA_DK), GLA_GATE_RANK ** -0.5),
        'gla_b_gate_f': nrm(ks[9], (L_, GLA_HEADS * GLA_DK), 0.1),
        'gla_w_gate_b': nrm(ks[10], (L_, GLA_GATE_RANK, GLA_HEADS * GLA_DK), GLA_GATE_RANK ** -0.5),
        'gla_b_gate_b': nrm(ks[11], (L_, GLA_HEADS * GLA_DK), 0.1),
        'gla_norm_g': gain(ks[12], (L_, GLA_HEADS, GLA_DV)),
        'w_proj_a': nrm(ks[13], (L_, W_A, D), W_A ** -0.5),
        'w_proj_b': nrm(ks[14], (L_, W_B, D), W_B ** -0.5),
        'w_proj_c': nrm(ks[15], (L_, W_C, D), W_C ** -0.5),
        'w_proj_d': nrm(ks[16], (L_, W_D, D), W_D ** -0.5),
        'w_merge': nrm(ks[17], (L_, D, N_BRANCH * D), D ** -0.5),
        'b_merge': nrm(ks[18], (L_, N_BRANCH * D), 0.02),
        'w_out': nrm(ks[19], (L_, D, D), D ** -0.5),
        'final_norm_g': gain(ks[20], (D,)),
    }


def reference(x, norm_g, w_in, mla_q_norm_g, mla_w_uq, mla_kv_norm_g, mla_w_ukv, nat_rpb,
              gla_w_gate_f, gla_b_gate_f, gla_w_gate_b, gla_b_gate_b, gla_norm_g,
              w_proj_a, w_proj_b, w_proj_c, w_proj_d, w_merge, b_merge, w_out, final_norm_g):
    pos = jnp.arange(x.shape[1], dtype=jnp.int32)
    for l in range(DEPTH):
        x = hybrid_layer(x, pos, norm_g[l], w_in[l], mla_q_norm_g[l], mla_w_uq[l],
                         mla_kv_norm_g[l], mla_w_ukv[l], nat_rpb[l],
                         gla_w_gate_f[l], gla_b_gate_f[l], gla_w_gate_b[l], gla_b_gate_b[l],
                         gla_norm_g[l], w_proj_a[l], w_proj_b[l], w_proj_c[l], w_proj_d[l],
                         w_merge[l], b_merge[l], w_out[l])
    return rms_norm(x, final_norm_g)
```

```python
import contextlib
import numpy as np
import concourse.bass as bass
import concourse.mybir as mybir
from concourse.bass_utils import run_bass_kernel_spmd

F32 = mybir.dt.float32
BF16 = mybir.dt.bfloat16
AF = mybir.ActivationFunctionType
ALU = mybir.AluOpType
_DTSZ = {F32: 4, BF16: 2}

N_DMA_SLOTS = 16
EPS = 1e-6
S_LEN = 4096
D = 1024
D_IN = 7136
OFF = dict(a_q=0, a_kv=256, a_kr=384, b=448, c_q=2752, c_k=3264, c_v=3776, d_q=4288, d_k=4544,
           d_v=4800, d_gf=5312, d_gb=5328, z_a=5344, z_b=5856, z_c=6112, z_d=6624)
DIL = ((128, 1), (512, 4), (2048, 16))
NEG = -30000.0


def _region(ap):
    if str(ap.space) == "DRAM":
        return None
    esz = _DTSZ[ap.dtype]
    pairs = ap.ap
    ps, pc = pairs[0]
    off = ap.offset
    p0 = off // ps
    f0 = off % ps
    lo = hi = f0
    for s, c in pairs[1:]:
        ext = (c - 1) * s
        if ext >= 0:
            hi += ext
        else:
            lo += ext
    name = ap.tensor.name
    if name == "PS":
        return (name, 0, 128, (lo * esz) // 2048 * 2048, ((hi + 1) * esz + 2047) // 2048 * 2048)
    return (name, p0, p0 + pc, lo * esz, (hi + 1) * esz)


class _Op:
    __slots__ = ("eng", "fn", "deps", "signal", "tick", "dma", "slot", "rnd", "final", "idx")

    def __init__(self, eng, fn, dma):
        self.eng = eng
        self.fn = fn
        self.deps = set()
        self.signal = False
        self.tick = 0
        self.dma = dma
        self.slot = -1
        self.rnd = 0
        self.final = False


class Sched:
    ENGS = ("pe", "act", "dve", "pool", "sp")
    DMAQ = ("sp", "act", "pool")

    def __init__(self, nc):
        self.nc = nc
        self.ops = []
        self.state = {}
        self.emitted = 0
        self.cnt = {e: 0 for e in self.ENGS}
        self.dcnt = {e: 0 for e in self.DMAQ}
        self.waited = {e: {} for e in self.ENGS}
        self.slot_last = {}
        self.nbar = 0
        self.finals = []
        self.es = None

    def _add(self, op, reads, writes):
        idx = len(self.ops)
        op.idx = idx
        deps = op.deps
        for ap in reads:
            r = _region(ap)
            if r is None:
                continue
            recs = self.state.setdefault(r[0], [])
            found = False
            psum = r[0] == "PS"
            for rec in recs:
                if rec[1] < r[4] and r[3] < rec[2] and rec[3] < r[2] and r[1] < rec[4]:
                    if rec[6] or (psum and rec[7][0] != op.eng):
                        deps.add(rec[5])
                    elif rec[7] == (op.eng, op.dma) and not op.dma and rec[1] == r[3] and rec[2] == r[4] and rec[3] == r[1] and rec[4] == r[2]:
                        rec[5] = idx
                        found = True
            if not found:
                recs.append([r[0], r[3], r[4], r[1], r[2], idx, False, (op.eng, op.dma)])
        for ap in writes:
            w = _region(ap)
            if w is None:
                continue
            recs = self.state.get(w[0], [])
            new = []
            for rec in recs:
                if rec[1] < w[4] and w[3] < rec[2] and rec[3] < w[2] and w[1] < rec[4]:
                    deps.add(rec[5])
                    if rec[1] >= w[3] and rec[2] <= w[4] and rec[3] >= w[1] and rec[4] <= w[2]:
                        continue
                new.append(rec)
            new.append([w[0], w[3], w[4], w[1], w[2], idx, True, (op.eng, op.dma)])
            self.state[w[0]] = new
        deps.discard(idx)
        self.ops.append(op)
        return op

    def op(self, eng, fn, reads=(), writes=()):
        return self._add(_Op(eng, fn, False), reads, writes)

    def dma(self, eng, fn, reads=(), writes=(), final=False):
        o = _Op(eng, fn, True)
        o.final = final
        return self._add(o, reads, writes)

    def begin(self):
        nc = self.nc
        self.es = contextlib.ExitStack()
        es = self.es
        self.esem = {e: es.enter_context(nc.semaphore("s_" + e)) for e in self.ENGS}
        self.dsem = {q: [es.enter_context(nc.semaphore("d_%s%d" % (q, i))) for i in range(N_DMA_SLOTS)] for q in self.DMAQ}
        self.bar = es.enter_context(nc.semaphore("s_bar"))
        self.block = es.enter_context(nc.Block())
        b = self.block
        self.handles = {"pe": b.tensor, "act": b.scalar, "dve": b.vector, "pool": b.gpsimd, "sp": b.sync}

    def flush(self):
        ops = self.ops
        new = ops[self.emitted:]
        if not new:
            return
        for o in new:
            if o.eng == "pe" and not o.dma:
                o.deps = {d for d in o.deps if not (ops[d].eng == "pe" and not ops[d].dma)}
            for d in o.deps:
                ops[d].signal = True
        last = {}
        for o in new:
            if not o.dma:
                last[o.eng] = o
        for o in last.values():
            o.signal = True
        for o in new:
            if o.dma:
                k = self.dcnt[o.eng]
                self.dcnt[o.eng] += 1
                o.slot = k % N_DMA_SLOTS
                o.rnd = k // N_DMA_SLOTS
                self.slot_last[(o.eng, o.slot)] = o.rnd
                if o.final:
                    self.finals.append(o)
            elif o.signal:
                self.cnt[o.eng] += 1
                o.tick = self.cnt[o.eng]
        esem, dsem = self.esem, self.dsem

        def make_body(E):
            waited = self.waited[E]

            def body(e):
                for o in new:
                    if o.eng != E:
                        continue
                    if o.dma and o.rnd > 0:
                        key = ("d", E, o.slot)
                        v = 16 * o.rnd
                        if waited.get(key, 0) < v:
                            e.wait_ge(dsem[E][o.slot], v)
                            waited[key] = v
                    for d in sorted(o.deps):
                        p = ops[d]
                        if p.dma:
                            key = ("d", p.eng, p.slot)
                            v = 16 * (p.rnd + 1)
                            s = dsem[p.eng][p.slot]
                        else:
                            key = ("e", p.eng)
                            v = p.tick
                            s = esem[p.eng]
                        if waited.get(key, 0) < v:
                            e.wait_ge(s, v)
                            waited[key] = v
                    ins = o.fn(e)
                    if o.dma:
                        ins.then_inc(dsem[E][o.slot], 16)
                    elif o.signal:
                        ins.then_inc(esem[E], 1)
                if E == "sp":
                    for E2 in self.ENGS:
                        v = self.cnt[E2]
                        if v > 0 and waited.get(("e", E2), 0) < v:
                            e.wait_ge(esem[E2], v)
                            waited[("e", E2)] = v
                    for (q, slot), rnd in self.slot_last.items():
                        v = 16 * (rnd + 1)
                        if waited.get(("d", q, slot), 0) < v:
                            e.wait_ge(dsem[q][slot], v)
                            waited[("d", q, slot)] = v
                    e.sem_inc(self.bar, 1)
                else:
                    e.wait_ge(self.bar, self.nbar + 1)
            return body

        for E in self.ENGS:
            self.handles[E](make_body(E))
        self.nbar += 1
        for E in self.ENGS:
            w = self.waited[E]
            for E2 in self.ENGS:
                w[("e", E2)] = self.cnt[E2]
            for (q, slot), rnd in self.slot_last.items():
                w[("d", q, slot)] = 16 * (rnd + 1)
        self.emitted = len(ops)
        self.state = {}
        for o in new:
            o.fn = None

    def end(self):
        self.flush()
        self.es.close()


class Builder:
    def __init__(self, n_layers=2, debug=False, stages="NABCDF"):
        self.n_layers = n_layers
        self.debug = debug
        self.stages = stages
        self.nc = bass.Bass("TRN2", target_bir_lowering=False)
        self.S = Sched(self.nc)
        self.names = {}

    @staticmethod
    def _aps(*xs):
        return [x for x in xs if hasattr(x, "ap") and hasattr(x, "tensor")]

    def mm(self, out, lhsT, rhs, start, stop):
        self.S.op("pe", lambda e: e.matmul(out, lhsT=lhsT, rhs=rhs, start=start, stop=stop, skip_group_check=True), [lhsT, rhs], [out])

    def tr(self, out, in_):
        ident = self.identb[:]
        self.S.op("pe", lambda e: e.transpose(out=out, in_=in_, identity=ident), [in_, ident], [out])

    def act(self, out, in_, func, **kw):
        rd = [in_] + self._aps(kw.get("bias"), kw.get("scale"))
        wr = [out] + self._aps(kw.get("accum_out"))
        self.S.op("act", lambda e: e.activation(out=out, in_=in_, func=func, **kw), rd, wr)

    def tt(self, out, in0, in1, op, eng="dve"):
        self.S.op(eng, lambda e: e.tensor_tensor(out=out, in0=in0, in1=in1, op=op), [in0, in1], [out])

    def ts(self, out, in0, s1, op0, s2=None, op1=None, eng="dve"):
        rd = [in0] + self._aps(s1, s2)
        if op1 is None:
            self.S.op(eng, lambda e: e.tensor_scalar(out=out, in0=in0, scalar1=s1, scalar2=None, op0=op0), rd, [out])
        else:
            self.S.op(eng, lambda e: e.tensor_scalar(out=out, in0=in0, scalar1=s1, scalar2=s2, op0=op0, op1=op1), rd, [out])

    def stt(self, out, in0, scalar, in1, op0, op1):
        rd = [in0, in1] + self._aps(scalar)
        self.S.op("dve", lambda e: e.scalar_tensor_tensor(out=out, in0=in0, scalar=scalar, in1=in1, op0=op0, op1=op1), rd, [out])

    def cp(self, out, in_, eng="dve"):
        if eng == "act":
            self.S.op("act", lambda e: e.activation(out=out, in_=in_, func=AF.Copy), [in_], [out])
        else:
            self.S.op(eng, lambda e: e.tensor_copy(out=out, in_=in_), [in_], [out])

    def recip(self, out, in_):
        self.S.op("dve", lambda e: e.reciprocal(out=out, in_=in_), [in_], [out])

    def memset(self, out, val, eng="dve"):
        self.S.op(eng, lambda e: e.memset(out, val), [], [out])

    def scan(self, out, d0, d1):
        self.S.op("dve", lambda e: e.tensor_tensor_scan(out=out, data0=d0, data1=d1, initial=0.0, op0=ALU.mult, op1=ALU.add), [d0, d1], [out])

    def dma(self, out, in_, q="sp", final=False):
        self.S.dma(q, lambda e: e.dma_start(out=out, in_=in_), [in_], [out], final=final)

    def cdma(self, out, in_):
        n = out.shape[-1]
        if n <= 2048:
            self.dma(out, in_, q="pool")
        else:
            for c0 in range(0, n, 2048):
                c1 = min(n, c0 + 2048)
                self.dma(out[..., c0:c1], in_[..., c0:c1], q="pool")

    def wload(self, dst, src2d):
        self.cdma(dst, src2d.rearrange("(c p) n -> p c n", p=128))

    def rstd(self, rs, ss):
        self.ts(rs, ss, EPS, ALU.add)
        self.act(rs, rs, AF.Sqrt)
        self.recip(rs, rs)

    def rstd_act(self, rs, ss, epsb):
        self.act(rs, ss, AF.Ln, bias=epsb)
        self.act(rs, rs, AF.Exp, scale=-0.5)

    def sb(self, es, name, shape, dt):
        n = self.names.get(name, 0)
        self.names[name] = n + 1
        return es.enter_context(self.nc.sbuf_tensor("%s_%d" % (name, n), list(shape), dt))

    def bank(self, b, n=1):
        return self.PS[:, b * 512:(b + n) * 512]

    def bank_bf(self, b, n=1):
        return self.PS[:, b * 512:(b + n) * 512].bitcast(BF16)

    def declare(self):
        nc = self.nc
        L = 2

        def din(name, shape, dt=F32):
            return nc.dram_tensor(name, list(shape), dt, kind="ExternalInput").ap()

        I = {}
        I["x"] = din("x", [S_LEN, D])
        I["norm_g"] = din("norm_g", [L, D])
        I["final_g"] = din("final_g", [D])
        I["w_in"] = din("w_in", [L, D, D_IN])
        I["w_in_rot"] = din("w_in_rot", [L, D, 64 + 1536])
        I["w_uq"] = din("w_uq", [L, 256, 768])
        I["w_uq_rot"] = din("w_uq_rot", [L, 256, 256])
        I["w_ukv"] = din("w_ukv", [L, 128, 1024])
        I["g3"] = din("g3", [L, 128, 3])
        I["nat_g"] = din("nat_g", [L, 8, 21, 128, 128])
        I["nat_mask"] = din("nat_mask", [21, 128, 128])
        I["w_gf"] = din("w_gf", [L, 16, 256])
        I["w_gb"] = din("w_gb", [L, 16, 256])
        I["b_gf"] = din("b_gf", [L, 128, 2])
        I["b_gb"] = din("b_gb", [L, 128, 2])
        I["gla_g"] = din("gla_g", [L, 128, 4])
        I["w_p"] = din("w_p", [L, 1792, D])
        I["w_merge"] = din("w_merge", [L, D, 4 * D])
        I["b_merge"] = din("b_merge", [L, 128, 32])
        I["w_out"] = din("w_out", [L, D, D])
        I["ropecos"] = din("ropecos", [128, S_LEN])
        I["ropesin"] = din("ropesin", [128, S_LEN])
        I["dil_mask"] = din("dil_mask", [128, 3 * 128])
        I["gla_mask"] = din("gla_mask", [128, 2 * 128])
        I["ident"] = din("ident", [128, 128])
        I["rotperm"] = din("rotperm", [128, 128])
        self.I = I
        kind_dbg = "ExternalOutput" if self.debug else "Internal"
        self.out = nc.dram_tensor("out", [S_LEN, D], F32, kind="ExternalOutput").ap()
        self.hT_d = nc.dram_tensor("hT_d", [128, 8, S_LEN], BF16, kind=kind_dbg).ap()
        self.oT_d = nc.dram_tensor("oT_d", [128, 14, S_LEN], BF16, kind=kind_dbg).ap()
        self.x1_d = nc.dram_tensor("x1_d", [S_LEN, D], F32, kind=kind_dbg).ap()

    def build(self):
        nc = self.nc
        self.declare()
        S = self.S
        with contextlib.ExitStack() as gs:
            self.PS = gs.enter_context(nc.psum_tensor("PS", [128, 4096], F32))
            self.identb = self.sb(gs, "identb", [128, 128], BF16)
            self.onesb = self.sb(gs, "onesb", [128, 128], BF16)
            S.begin()
            self.cdma(self.identb[:], self.I["ident"])
            self.memset(self.onesb[:], 1.0)
            S.flush()
            for l in range(self.n_layers):
                last = l == self.n_layers - 1
                x_src = self.I["x"] if l == 0 else self.x1_d
                x_dst = self.out if last else self.x1_d
                with contextlib.ExitStack() as ls:
                    self.hT = self.sb(ls, "hT", [128, 8, S_LEN], BF16)
                    if "N" in self.stages:
                        self.stage_norm(l, x_src)
                    if "A" in self.stages:
                        self.stage_mla(l)
                    if "B" in self.stages:
                        self.stage_dil(l)
                    if "C" in self.stages:
                        self.stage_nat(l)
                if "D" in self.stages:
                    self.stage_gla(l)
                if "F" in self.stages:
                    self.stage_final(l, x_src, x_dst, last)
            S.end()
        return nc

    def stage_norm(self, l, x_src):
        hT = self.hT
        with contextlib.ExitStack() as es:
            NB_ = 4
            xt = [self.sb(es, "xt", [128, D], F32) for _ in range(NB_)]
            sq = self.sb(es, "sq", [128, D], BF16)
            gb = self.sb(es, "gb", [128, D], F32)
            ss = [self.sb(es, "ss", [128, 1], F32) for _ in range(NB_)]
            rs = [self.sb(es, "rs", [128, 1], F32) for _ in range(NB_)]
            xn = [self.sb(es, "xn", [128, D], BF16) for _ in range(NB_)]
            epsb = self.sb(es, "epsb", [128, 1], F32)
            self.memset(epsb[:], EPS)
            self.dma(gb[:], self.I["norm_g"][l].partition_broadcast(128))
            units = []
            for tt in range(32):
                def p1(tt=tt):
                    i = tt % NB_
                    tok = slice(tt * 128, (tt + 1) * 128)
                    self.dma(xt[i][:], x_src[tok, :])
                    self.act(sq[:], xt[i][:], AF.Square, scale=1.0 / 32.0, accum_out=ss[i][:])
                    self.rstd_act(rs[i][:], ss[i][:], epsb[:])
                    self.stt(xn[i][:], xt[i][:], rs[i][:], gb[:], ALU.mult, ALU.mult)

                def p3(tt=tt):
                    i = tt % NB_
                    tok = slice(tt * 128, (tt + 1) * 128)
                    pT = self.bank_bf(tt % 4)
                    for c in range(8):
                        self.tr(pT[:, c * 128:(c + 1) * 128], xn[i][:, c * 128:(c + 1) * 128])
                    self.cp(hT[:, :, tok], pT.rearrange("p (c t) -> p c t", c=8), eng=("act" if tt % 2 == 0 else "dve"))
                units.append((p1, lambda: None, p3))
            self.pipeline(units, 2)
            for tq in range(8):
                tk = slice(tq * 512, (tq + 1) * 512)
                self.dma(self.hT_d[:, :, tk], hT[:, :, tk])
            self.S.flush()

    @staticmethod
    def pipeline(units, la):
        n = len(units)
        for k in range(n + la):
            if k < n:
                units[k][0]()
                units[k][1]()
            if k - la >= 0:
                units[k - la][2]()

    def rope_combine(self, dst, psA, psB, cosv, sinv, t1, t2):
        self.tt(t1, psA, cosv, ALU.mult)
        self.tt(t2, psB, sinv, ALU.mult)
        self.tt(dst, t1, t2, ALU.add)

    def stage_mla(self, l):
        S = self.S
        I = self.I
        hT = self.hT
        SCALE = float((128 + 64) ** -0.5)
        with contextlib.ExitStack() as es:
            w_lat = self.sb(es, "w_lat", [128, 8, 384], BF16)
            w_kr = self.sb(es, "w_kr", [128, 8, 64], BF16)
            w_krr = self.sb(es, "w_krr", [128, 8, 64], BF16)
            w_uqb = self.sb(es, "w_uqb", [128, 2, 768], BF16)
            w_uqr = self.sb(es, "w_uqr", [128, 2, 256], BF16)
            w_ukvb = self.sb(es, "w_ukvb", [128, 1024], BF16)
            cosb = self.sb(es, "cosb", [128, S_LEN], BF16)
            sinb = self.sb(es, "sinb", [128, S_LEN], BF16)
            g3 = self.sb(es, "g3", [128, 3], F32)
            latT = self.sb(es, "latT", [128, 3, S_LEN], BF16)
            kpeT = self.sb(es, "kpeT", [128, S_LEN], BF16)
            self.wload(w_lat[:], I["w_in"][l][:, 0:384])
            self.wload(w_kr[:], I["w_in"][l][:, 384:448])
            self.wload(w_krr[:], I["w_in_rot"][l][:, 0:64])
            self.wload(w_uqb[:], I["w_uq"][l])
            self.wload(w_uqr[:], I["w_uq_rot"][l])
            self.cdma(w_ukvb[:], I["w_ukv"][l])
            self.cdma(cosb[:], I["ropecos"])
            self.cdma(sinb[:], I["ropesin"])
            self.dma(g3[:], I["g3"][l])
            for es1 in (es,):
                ss2 = [self.sb(es1, "ss2", [128, 2], F32) for _ in range(3)]
                rs2 = [self.sb(es1, "rs2", [128, 2], F32) for _ in range(3)]
                junk = self.sb(es1, "junk", [128, 256], BF16)
                latn = [self.sb(es1, "latn", [128, 384], BF16) for _ in range(3)]
                units = []
                for tt in range(32):
                    def p1(tt=tt):
                        tok = slice(tt * 128, (tt + 1) * 128)
                        psL = self.bank(tt % 3)
                        for kc in range(8):
                            self.mm(psL[:, 0:384], hT[:, kc, tok], w_lat[:, kc, :], kc == 0, kc == 7)

                    def p2(tt=tt):
                        i = tt % 3
                        psL = self.bank(tt % 3)
                        self.act(junk[:, 0:256], psL[:, 0:256], AF.Square, scale=1.0 / 16.0, accum_out=ss2[i][:, 0:1])
                        self.act(junk[:, 0:128], psL[:, 256:384], AF.Square, scale=float(128 ** -0.5), accum_out=ss2[i][:, 1:2])
                        self.rstd(rs2[i][:], ss2[i][:])
                        self.ts(latn[i][:, 0:256], psL[:, 0:256], rs2[i][:, 0:1], ALU.mult)
                        self.ts(latn[i][:, 256:384], psL[:, 256:384], rs2[i][:, 1:2], ALU.mult)

                    def p3(tt=tt):
                        i = tt % 3
                        tok = slice(tt * 128, (tt + 1) * 128)
                        pT = self.bank_bf(4 + tt % 2)
                        for c in range(3):
                            self.tr(pT[:, c * 128:(c + 1) * 128], latn[i][:, c * 128:(c + 1) * 128])
                        for c in range(3):
                            self.ts(latT[:, c, tok], pT[:, c * 128:(c + 1) * 128], g3[:, c:c + 1], ALU.mult)
                    units.append((p1, p2, p3))
                self.pipeline(units, 2)
            for es2 in (es,):
                t1 = [self.sb(es2, "t1", [128, 512], F32) for _ in range(2)]
                t2 = [self.sb(es2, "t2", [128, 512], F32) for _ in range(2)]
                self.memset(kpeT[64:128, :], 0.0)
                for tq in range(8):
                    i = tq % 2
                    tk = slice(tq * 512, (tq + 1) * 512)
                    pa = self.bank(2 * i)
                    pb = self.bank(2 * i + 1)
                    for kc in range(8):
                        self.mm(pa[0:64, :], w_kr[:, kc, :], hT[:, kc, tk], kc == 0, kc == 7)
                    for kc in range(8):
                        self.mm(pb[0:64, :], w_krr[:, kc, :], hT[:, kc, tk], kc == 0, kc == 7)
                    self.rope_combine(kpeT[0:64, tk], pa[0:64, :], pb[0:64, :], cosb[0:64, tk], sinb[0:64, tk], t1[i][0:64, :], t2[i][0:64, :])
            for es3 in (es,):
                qnT = self.sb(es3, "qnT", [128, S_LEN], BF16)
                qpT = self.sb(es3, "qpT", [128, S_LEN], BF16)
                knT = self.sb(es3, "knT", [128, S_LEN], BF16)
                vh = self.sb(es3, "vh", [128, 32, 128], BF16)
                w_za = self.sb(es3, "w_za", [128, 8, 128], BF16)
                t1 = self.sb(es3, "t1", [128, 512], F32)
                t2 = self.sb(es3, "t2", [128, 512], F32)
                pt = [self.sb(es3, "pt", [128, 512], BF16) for _ in range(8)]
                acc = [self.sb(es3, "acc", [128, 512], F32) for _ in range(2)]
                onesf = self.sb(es3, "onesf", [128, 128], F32)
                psum2 = [self.sb(es3, "psum2", [128, 512], BF16) for _ in range(2)]
                self.memset(onesf[:], 1.0)
                zg = [self.sb(es3, "zg", [128, 512], BF16) for _ in range(2)]
                rec = self.sb(es3, "rec", [128, 512], F32)
                ot = self.sb(es3, "ot", [128, 512], F32)
                og = [self.sb(es3, "og", [128, 512], BF16) for _ in range(2)]
                self.memset(qpT[64:128, :], 0.0)
                for h in range(4):
                    self.wload(w_za[:], I["w_in"][l][:, OFF["z_a"] + h * 128: OFF["z_a"] + (h + 1) * 128])
                    for tq in range(8):
                        tk = slice(tq * 512, (tq + 1) * 512)
                        i = tq % 2
                        pp = self.bank(i)
                        for kc in range(2):
                            self.mm(pp, w_uqb[:, kc, h * 192: h * 192 + 128], latT[:, kc, tk], kc == 0, kc == 1)
                        self.cp(qnT[:, tk], pp, eng="act")
                        pa, pb = self.bank(2), self.bank(3)
                        for kc in range(2):
                            self.mm(pa[0:64, :], w_uqb[:, kc, h * 192 + 128: h * 192 + 192], latT[:, kc, tk], kc == 0, kc == 1)
                        for kc in range(2):
                            self.mm(pb[0:64, :], w_uqr[:, kc, h * 64:(h + 1) * 64], latT[:, kc, tk], kc == 0, kc == 1)
                        self.rope_combine(qpT[0:64, tk], pa[0:64, :], pb[0:64, :], cosb[0:64, tk], sinb[0:64, tk], t1[0:64, :], t2[0:64, :])
                    for tq in range(8):
                        tk = slice(tq * 512, (tq + 1) * 512)
                        i = tq % 2
                        pp = self.bank(i)
                        self.mm(pp, w_ukvb[:, h * 256: h * 256 + 128], latT[:, 2, tk], True, True)
                        self.cp(knT[:, tk], pp, eng="act")
                    for t4 in range(8):
                        i = t4 % 2
                        pp = self.bank(i)
                        for j in range(4):
                            tt = t4 * 4 + j
                            self.mm(pp[:, j * 128:(j + 1) * 128], latT[:, 2, tt * 128:(tt + 1) * 128], w_ukvb[:, h * 256 + 128: h * 256 + 256], True, True)
                        self.cp(vh[:, t4 * 4:(t4 + 1) * 4, :], pp.rearrange("p (j d) -> p j d", j=4), eng="dve")
                    units = []
                    for tq in range(8):
                        for kt in range(32):
                            cnt = tq * 32 + kt

                            def p1(tq=tq, kt=kt, cnt=cnt):
                                tk = slice(tq * 512, (tq + 1) * 512)
                                if kt == 0:
                                    pz = self.bank(7)
                                    for kc in range(8):
                                        self.mm(pz, w_za[:, kc, :], hT[:, kc, tk], kc == 0, kc == 7)
                                    self.act(zg[tq % 2][:], pz, AF.Silu)
                                ks = slice(kt * 128, (kt + 1) * 128)
                                pS = self.bank(cnt % 4)
                                self.mm(pS, knT[:, ks], qnT[:, tk], True, False)
                                self.mm(pS, kpeT[:, ks], qpT[:, tk], False, True)

                            def p2(cnt=cnt):
                                self.act(pt[cnt % 8][:], self.bank(cnt % 4), AF.Exp, scale=SCALE)

                            def p3(tq=tq, kt=kt, cnt=cnt):
                                tk = slice(tq * 512, (tq + 1) * 512)
                                o = tq % 2
                                pO = self.bank(4 + o)
                                self.mm(pO, vh[:, kt, :], pt[cnt % 8][:], kt == 0, kt == 31)
                                if kt % 2 == 1:
                                    pa_, pb_ = pt[(cnt - 1) % 8], pt[cnt % 8]
                                    if kt == 1:
                                        self.tt(acc[o][:], pa_[:], pb_[:], ALU.add)
                                    else:
                                        s01 = psum2[(cnt // 2) % 2]
                                        self.tt(s01[:], pa_[:], pb_[:], ALU.add)
                                        self.tt(acc[o][:], acc[o][:], s01[:], ALU.add)
                                if kt == 31:
                                    pD = self.bank(6)
                                    self.mm(pD, onesf[:], acc[o][:], True, True)
                                    self.act(rec[:], pD, AF.Ln)
                                    self.act(rec[:], rec[:], AF.Exp, scale=-1.0)
                                    self.tt(ot[:], pO, rec[:], ALU.mult)
                                    self.tt(og[o][:], ot[:], zg[o][:], ALU.mult)
                                    self.dma(self.oT_d[:, h, tk], og[o][:])
                            units.append((p1, p2, p3))
                    self.pipeline(units, 3)
                S.flush()

    def stage_dil(self, l):
        S = self.S
        I = self.I
        hT = self.hT
        with contextlib.ExitStack() as es:
            cosb = self.sb(es, "cosb", [128, S_LEN], BF16)
            sinb = self.sb(es, "sinb", [128, S_LEN], BF16)
            dmask = self.sb(es, "dmask", [128, 3, 128], BF16)
            numT = self.sb(es, "numT", [128, S_LEN], F32)
            denT = self.sb(es, "denT", [128, S_LEN], F32)
            zT2 = [self.sb(es, "zT", [128, S_LEN], BF16) for _ in range(2)]
            qT = self.sb(es, "qT", [128, S_LEN], BF16)
            kT = self.sb(es, "kT", [128, S_LEN], BF16)
            vg = self.sb(es, "vg", [128, 32, 128], BF16)
            wq = self.sb(es, "wq", [128, 8, 128], BF16)
            wk = self.sb(es, "wk", [128, 8, 128], BF16)
            rotp = self.sb(es, "rotp", [128, 128], BF16)
            xbs = [self.sb(es, "xbs", [128, 512], BF16) for _ in range(2)]
            tmpb = [self.sb(es, "tmpb", [128, 512], BF16) for _ in range(2)]
            wv = self.sb(es, "wv", [128, 8, 128], BF16)
            wz = self.sb(es, "wz", [128, 8, 128], BF16)
            t1 = self.sb(es, "t1", [128, 512], F32)
            t2 = self.sb(es, "t2", [128, 512], F32)
            pt = [self.sb(es, "ptd", [128, 2, 3, 128], BF16) for _ in range(3)]
            og = [self.sb(es, "ogd", [128, 512], BF16) for _ in range(2)]
            self.cdma(cosb[:], I["ropecos"])
            self.cdma(sinb[:], I["ropesin"])
            self.cdma(dmask[:], I["dil_mask"].rearrange("p (s n) -> p s n", s=3))
            self.cdma(rotp[:], I["rotperm"])
            win = I["w_in"][l]
            for hp in range(2):
                zT = zT2[hp]
                zc = OFF["z_b"] + hp * 128
                self.wload(wz[:], win[:, zc:zc + 128])
                for tq in range(8):
                    tk = slice(tq * 512, (tq + 1) * 512)
                    i = tq % 2
                    pz = self.bank(i)
                    for kc in range(8):
                        self.mm(pz, wz[:, kc, :], hT[:, kc, tk], kc == 0, kc == 7)
                    self.act(zT[:, tk], pz, AF.Silu)
                for g, (window, r) in enumerate(DIL):
                    Lg = S_LEN // r
                    nb = Lg // 128
                    cq = OFF["b"] + g * 768 + hp * 128
                    ck = cq + 256
                    cv = cq + 512
                    rq = 64 + g * 512 + hp * 128
                    rk = rq + 256
                    self.wload(wq[:], win[:, cq:cq + 128])
                    self.wload(wk[:], win[:, ck:ck + 128])
                    self.wload(wv[:], win[:, cv:cv + 128])
                    units = []
                    for di_, (dst, wa) in enumerate(((qT, wq), (kT, wk))):
                        for tq in range(8):
                            n_ = di_ * 8 + tq

                            def p1(tq=tq, n_=n_, wa=wa):
                                tk = slice(tq * 512, (tq + 1) * 512)
                                pa = self.bank(0 + 2 * (n_ % 2))
                                for kc in range(8):
                                    self.mm(pa, wa[:, kc, :], hT[:, kc, tk], kc == 0, kc == 7)
                                self.cp(xbs[n_ % 2][:], pa, eng="act")

                            def p3(tq=tq, n_=n_, dst=dst, r=r):
                                tk = slice(tq * 512, (tq + 1) * 512)
                                pa, pb = self.bank(0 + 2 * (n_ % 2)), self.bank(1 + 2 * (n_ % 2))
                                self.mm(pb, rotp[:], xbs[n_ % 2][:], True, True)
                                self.tt(t1[:], pa, cosb[:, tk], ALU.mult)
                                self.tt(t2[:], pb, sinb[:, tk], ALU.mult)
                                if r == 1:
                                    self.tt(dst[:, tk], t1[:], t2[:], ALU.add)
                                else:
                                    tb = tmpb[n_ % 2]
                                    self.tt(tb[:], t1[:], t2[:], ALU.add)
                                    ni = 512 // r
                                    i0 = tq * ni
                                    dview = dst[:].rearrange("p (m i) -> p i m", m=r)[:, i0:i0 + ni, :]
                                    self.cp(dview, tb[:].rearrange("p (i m) -> p i m", m=r), eng="pool")
                            units.append((p1, lambda: None, p3))
                    self.pipeline(units, 1)
                    for t4 in range(8):
                        i = 4 + t4 % 2
                        pv = self.bank(i)
                        for j in range(4):
                            ti = t4 * 4 + j
                            m, c = ti // nb, ti % nb
                            t0 = m + r * 128 * c
                            for kc in range(8):
                                self.mm(pv[:, j * 128:(j + 1) * 128], hT[:, kc, t0: t0 + 127 * r + 1: r], wv[:, kc, :], kc == 0, kc == 7)
                        self.cp(vg[:, t4 * 4:(t4 + 1) * 4, :], pv.rearrange("p (j d) -> p j d", j=4), eng="act")
                    units = []
                    for qd in range(8):
                        for bi in range(4):
                            def geom(qd=qd, bi=bi):
                                idx = qd * 4 + bi
                                m, j = idx // nb, idx % nb
                                slots = [s_ for s_ in range(3) if 0 <= j - 1 + s_ < nb]
                                return idx, m, j, slots

                            def p1(qd=qd, bi=bi, geom=geom):
                                idx, m, j, slots = geom()
                                u = idx % 2
                                s_lo, s_hi = slots[0], slots[-1]
                                pSh = [self.bank(2 * u + hd).rearrange("p (s n) -> p s n", n=128) for hd in range(2)]
                                qs = slice(idx * 128, (idx + 1) * 128)
                                for hd in range(2):
                                    self.mm(pSh[hd][:, s_lo:s_hi + 1, :], self.identb[:], dmask[:, s_lo:s_hi + 1, :], True, False)
                                for s_ in slots:
                                    c = j - 1 + s_
                                    ks = slice((m * nb + c) * 128, (m * nb + c + 1) * 128)
                                    for hd in range(2):
                                        rows = slice(hd * 64, (hd + 1) * 64)
                                        self.mm(pSh[hd][:, s_, :], kT[rows, ks], qT[rows, qs], False, True)

                            def p2(qd=qd, bi=bi, geom=geom):
                                idx, m, j, slots = geom()
                                u = idx % 2
                                s_lo, s_hi = slots[0], slots[-1]
                                for hd in range(2):
                                    pSh = self.bank(2 * u + hd).rearrange("p (s n) -> p s n", n=128)
                                    self.act(pt[idx % 3][:, hd, s_lo:s_hi + 1, :], pSh[:, s_lo:s_hi + 1, :], AF.Exp, scale=0.125)

                            def p3(qd=qd, bi=bi, geom=geom, g=g, r=r, Lg=Lg):
                                idx, m, j, slots = geom()
                                s_lo, s_hi = slots[0], slots[-1]
                                o = qd % 2
                                pO = self.bank(4 + o)
                                pD = self.bank(6 + o)
                                ptb = pt[idx % 3]
                                for hd in range(2):
                                    rows = slice(hd * 64, (hd + 1) * 64)
                                    for s_ in slots:
                                        c = j - 1 + s_
                                        self.mm(pO[rows, bi * 128:(bi + 1) * 128], vg[:, m * nb + c, hd * 64:(hd + 1) * 64], ptb[:, hd, s_, :], s_ == s_lo, s_ == s_hi)
                                    for s_ in slots:
                                        self.mm(pD[rows, bi * 128:(bi + 1) * 128], self.onesb[:, 0:64], ptb[:, hd, s_, :], s_ == s_lo, s_ == s_hi)
                                if bi != 3:
                                    return
                                p0 = qd * 512
                                if r == 1:
                                    nview, dview, po_v, pd_v = numT[:, p0:p0 + 512], denT[:, p0:p0 + 512], pO, pD
                                elif Lg >= 512:
                                    m0, i0_ = p0 // Lg, p0 % Lg
                                    nview = numT[:].rearrange("p (i m) -> p m i", m=r)[:, m0, i0_:i0_ + 512]
                                    dview = denT[:].rearrange("p (i m) -> p m i", m=r)[:, m0, i0_:i0_ + 512]
                                    po_v, pd_v = pO, pD
                                else:
                                    nm_ = 512 // Lg
                                    m0 = p0 // Lg
                                    nview = numT[:].rearrange("p (i m) -> p m i", m=r)[:, m0:m0 + nm_, :]
                                    dview = denT[:].rearrange("p (i m) -> p m i", m=r)[:, m0:m0 + nm_, :]
                                    po_v = pO.rearrange("p (a b) -> p a b", a=nm_)
                                    pd_v = pD.rearrange("p (a b) -> p a b", a=nm_)
                                if g == 0:
                                    self.cp(nview, po_v, eng="dve")
                                    self.cp(dview, pd_v, eng="dve")
                                else:
                                    self.tt(nview, po_v, nview, ALU.add)
                                    self.tt(dview, pd_v, dview, ALU.add)
                            units.append((p1, p2, p3))
                    self.pipeline(units, 1)
                for tq in range(8):
                    tk = slice(tq * 512, (tq + 1) * 512)
                    o = tq % 2
                    self.act(denT[:, tk], denT[:, tk], AF.Ln)
                    self.act(denT[:, tk], denT[:, tk], AF.Exp, scale=-1.0)
                    self.tt(numT[:, tk], numT[:, tk], denT[:, tk], ALU.mult)
                    self.tt(og[o][:], numT[:, tk], zT[:, tk], ALU.mult)
                    self.dma(self.oT_d[:, 4 + hp, tk], og[o][:])
            S.flush()

    @staticmethod
    def nat_seq():
        n = 5
        info = {}
        for b in range(32):
            rs0 = min(max(2 * b - 4, 0), 56)
            rs1 = min(max(2 * b + 1 - 4, 0), 56)
            a_lo, a_hi = rs0 // 2, (rs1 + 7) // 2
            cnt = a_hi - a_lo + 1
            if 2 <= b <= 29:
                assert a_lo == b - 2 and cnt == 5
                info[b] = (0, a_lo, cnt)
            else:
                info[b] = (n, a_lo, cnt)
                n += cnt
        return info, n

    def stage_nat(self, l):
        S = self.S
        I = self.I
        hT = self.hT
        info, ntile = self.nat_seq()
        assert ntile == 21
        with contextlib.ExitStack() as es:
            nmask = self.sb(es, "nmask", [128, 21, 128], BF16)
            nb_sb = self.sb(es, "nb_sb", [128, 2, 21, 128], BF16)
            zT = self.sb(es, "zT", [128, S_LEN], BF16)
            qT = self.sb(es, "qTz", [128, 2, S_LEN], BF16)
            kT = self.sb(es, "kT", [128, S_LEN], BF16)
            vt = self.sb(es, "vt", [128, 32, 128], BF16)
            self.memset(qT[64:128, 0, :], 0.0)
            self.memset(qT[0:64, 1, :], 0.0)
            wq = self.sb(es, "wq", [128, 8, 128], BF16)
            wk = self.sb(es, "wk", [128, 8, 128], BF16)
            wv = self.sb(es, "wv", [128, 8, 128], BF16)
            wz = self.sb(es, "wz", [128, 8, 128], BF16)
            pt = [self.sb(es, "ptn", [128, 5, 128], BF16) for _ in range(4)]
            rec = self.sb(es, "rec", [128, 512], F32)
            ot = self.sb(es, "ot", [128, 512], F32)
            og = [self.sb(es, "ogn", [128, 512], BF16) for _ in range(2)]
            self.cdma(nmask[:], I["nat_mask"].rearrange("t k q -> k t q"))
            win = I["w_in"][l]
            for hp in range(4):
                for (wt, off) in ((wq, OFF["c_q"]), (wk, OFF["c_k"]), (wv, OFF["c_v"]), (wz, OFF["z_c"])):
                    self.wload(wt[:], win[:, off + hp * 128: off + (hp + 1) * 128])
                for hd in range(2):
                    self.cdma(nb_sb[:, hd, :, :], I["nat_g"][l, 2 * hp + hd].rearrange("t k q -> k t q"))
                    self.tt(nb_sb[:, hd, :, :], nb_sb[:, hd, :, :], nmask[:], ALU.add)
                for tq in range(8):
                    tk = slice(tq * 512, (tq + 1) * 512)
                    for n_, wt in enumerate((wq, wk, wz)):
                        pp = self.bank(n_ + 3 * (tq % 2))
                        for kc in range(8):
                            self.mm(pp, wt[:, kc, :], hT[:, kc, tk], kc == 0, kc == 7)
                        if n_ == 0:
                            self.act(qT[0:64, 0, tk], pp[0:64, :], AF.Copy, scale=0.125)
                            self.act(qT[64:128, 1, tk], pp[64:128, :], AF.Copy, scale=0.125)
                        elif n_ == 1:
                            self.cp(kT[:, tk], pp, eng="dve")
                        else:
                            self.act(zT[:, tk], pp, AF.Silu)
                for t4 in range(8):
                    i = 6 + t4 % 2
                    pv = self.bank(i)
                    for j in range(4):
                        tt_ = t4 * 4 + j
                        for kc in range(8):
                            self.mm(pv[:, j * 128:(j + 1) * 128], hT[:, kc, tt_ * 128:(tt_ + 1) * 128], wv[:, kc, :], kc == 0, kc == 7)
                    self.cp(vt[:, t4 * 4:(t4 + 1) * 4, :], pv.rearrange("p (j d) -> p j d", j=4), eng="dve")
                units = []
                for qd in range(8):
                    for bi in range(4):
                        for hd in range(2):
                            def p1(qd=qd, bi=bi, hd=hd):
                                b = qd * 4 + bi
                                base, a_lo, ns = info[b]
                                qs = slice(b * 128, (b + 1) * 128)
                                rows = slice(hd * 64, (hd + 1) * 64)
                                pS5 = self.bank(2 * hd, 2).rearrange("p (s n) -> p s n", n=128)
                                n0 = min(ns, 4)
                                self.mm(pS5[:, 0:n0, :], self.identb[:], nb_sb[:, hd, base: base + n0, :], True, False)
                                if ns == 5:
                                    self.mm(pS5[:, 4:5, :], self.identb[:], nb_sb[:, hd, base + 4: base + 5, :], True, False)
                                for s_ in range(ns):
                                    a = a_lo + s_
                                    self.mm(pS5[:, s_, :], kT[:, a * 128:(a + 1) * 128], qT[:, hd, qs], False, True)

                            def p2(qd=qd, bi=bi, hd=hd):
                                b = qd * 4 + bi
                                ns = info[b][2]
                                pS5 = self.bank(2 * hd, 2).rearrange("p (s n) -> p s n", n=128)
                                self.act(pt[2 * hd + b % 2][:, 0:ns, :], pS5[:, 0:ns, :], AF.Exp)

                            def p3(qd=qd, bi=bi, hd=hd):
                                b = qd * 4 + bi
                                base, a_lo, ns = info[b]
                                o = qd % 2
                                pO = self.bank(4 + o)
                                pD = self.bank(6 + o)
                                rows = slice(hd * 64, (hd + 1) * 64)
                                ptb = pt[2 * hd + b % 2]
                                for s_ in range(ns):
                                    a = a_lo + s_
                                    self.mm(pO[rows, bi * 128:(bi + 1) * 128], vt[:, a, hd * 64:(hd + 1) * 64], ptb[:, s_, :], s_ == 0, s_ == ns - 1)
                                for s_ in range(ns):
                                    self.mm(pD[rows, bi * 128:(bi + 1) * 128], self.onesb[:, 0:64], ptb[:, s_, :], s_ == 0, s_ == ns - 1)
                                if bi == 3 and hd == 1:
                                    tk = slice(qd * 512, (qd + 1) * 512)
                                    self.act(rec[:], pD, AF.Ln)
                                    self.act(rec[:], rec[:], AF.Exp, scale=-1.0)
                                    self.tt(ot[:], pO, rec[:], ALU.mult)
                                    self.tt(og[o][:], ot[:], zT[:, tk], ALU.mult)
                                    self.dma(self.oT_d[:, 6 + hp, tk], og[o][:])
                            units.append((p1, p2, p3))
                self.pipeline(units, 1)
            S.flush()

    def stage_gla(self, l):
        S = self.S
        I = self.I
        win = I["w_in"][l]
        with contextlib.ExitStack() as es:
            gmask = self.sb(es, "gmask", [128, 2, 128], BF16)
            w_g = self.sb(es, "w_g", [64, 256], BF16)
            nbias = self.sb(es, "nbias", [128, 2, 2], F32)
            gg = self.sb(es, "gg", [128, 4], F32)
            q_sb = self.sb(es, "q_sb", [128, S_LEN], BF16)
            k_sb = self.sb(es, "k_sb", [128, S_LEN], BF16)
            z_sb = self.sb(es, "z_sb", [128, 2, S_LEN], BF16)
            v_sb = self.sb(es, "v_sb", [128, 32, 256], BF16)
            g_sb = self.sb(es, "g_sb", [64, S_LEN], BF16)
            uu = [self.sb(es, "uu", [128, S_LEN], F32) for _ in range(2)]
            blast2 = self.sb(es, "blast2", [128, 2, 32], F32)
            dec2 = self.sb(es, "dec2", [128, 2, 32], F32)
            umid2 = self.sb(es, "umid2", [128, 2, 32], F32)
            self.cdma(gmask[:], I["gla_mask"].rearrange("p (s n) -> p s n", s=2))
            self.cdma(w_g[0:16, :], I["w_gf"][l])
            self.cdma(w_g[32:48, :], I["w_gb"][l])
            self.dma(nbias[:, 0, :], I["b_gf"][l])
            self.dma(nbias[:, 1, :], I["b_gb"][l])
            self.ts(nbias[:], nbias[:], -1.0, ALU.mult)
            self.dma(gg[:], I["gla_g"][l])
            for hp in range(2):
                with contextlib.ExitStack() as e1:
                    wq = self.sb(e1, "wq", [128, 8, 128], BF16)
                    wk = self.sb(e1, "wk", [128, 8, 128], BF16)
                    wv = self.sb(e1, "wv", [128, 8, 256], BF16)
                    wz = self.sb(e1, "wz", [128, 8, 256], BF16)
                    wg = self.sb(e1, "wg", [128, 8, 32], BF16)
                    ht = [self.sb(e1, "ht", [128, 8, 512], BF16) for _ in range(2)]
                    la_t = [self.sb(e1, "la_t", [128, 512], F32) for _ in range(2)]
                    self.wload(wq[:], win[:, OFF["d_q"] + hp * 128: OFF["d_q"] + (hp + 1) * 128])
                    self.wload(wk[:], win[:, OFF["d_k"] + hp * 128: OFF["d_k"] + (hp + 1) * 128])
                    self.wload(wv[:], win[:, OFF["d_v"] + hp * 256: OFF["d_v"] + (hp + 1) * 256])
                    self.wload(wz[:], win[:, OFF["z_d"] + hp * 256: OFF["z_d"] + (hp + 1) * 256])
                    if hp == 0:
                        self.wload(wg[:], win[:, OFF["d_gf"]:OFF["d_gf"] + 32])
                    for tq in range(8):
                        tk = slice(tq * 512, (tq + 1) * 512)
                        hb = ht[tq % 2]
                        self.dma(hb[:], self.hT_d[:, :, tk])

                        def proj(out_ap, lhs_of_kc):
                            for kc in range(8):
                                self.mm(out_ap, lhs_of_kc(kc), hb[:, kc, :], kc == 0, kc == 7)
                        proj(self.bank(0), lambda kc: wq[:, kc, :])
                        self.act(q_sb[:, tk], self.bank(0), AF.Copy, scale=0.125)
                        proj(self.bank(1), lambda kc: wk[:, kc, :])
                        self.cp(k_sb[:, tk], self.bank(1), eng="dve")
                        for e_ in range(2):
                            proj(self.bank(2 + e_), lambda kc: wz[:, kc, e_ * 128:(e_ + 1) * 128])
                            self.act(z_sb[:, e_, tk], self.bank(2 + e_), AF.Silu)
                        if hp == 0:
                            proj(self.bank(4)[0:16, :], lambda kc: wg[:, kc, 0:16])
                            self.cp(g_sb[0:16, tk], self.bank(4)[0:16, :], eng="dve")
                            proj(self.bank(5)[32:48, :], lambda kc: wg[:, kc, 16:32])
                            self.cp(g_sb[32:48, tk], self.bank(5)[32:48, :], eng="dve")
                        for s2 in range(2):
                            pv = self.bank(6 + s2)
                            for j in range(2):
                                sub = s2 * 2 + j
                                for kc in range(8):
                                    self.mm(pv[:, j * 256:(j + 1) * 256], hb[:, kc, sub * 128:(sub + 1) * 128], wv[:, kc, :], kc == 0, kc == 7)
                            self.cp(v_sb[:, tq * 4 + s2 * 2: tq * 4 + s2 * 2 + 2, :], pv.rearrange("p (j d) -> p j d", j=2), eng="act")
                        for dr in range(2):
                            grow = slice(0, 16) if dr == 0 else slice(32, 48)
                            pp = self.bank(4 + dr)
                            lt = la_t[dr]
                            self.mm(pp, w_g[grow, hp * 128:(hp + 1) * 128], g_sb[grow, tk], True, True)
                            self.act(lt[:], pp, AF.Exp, scale=-1.0, bias=nbias[:, dr, hp:hp + 1])
                            self.act(lt[:], lt[:], AF.Ln, bias=1.0)
                            self.ts(lt[:], lt[:], -1.0 / 16.0, ALU.mult)
                            for c4 in range(4):
                                c = tq * 4 + c4
                                self.scan(uu[dr][:, c * 128:(c + 1) * 128], self.onesb[:, :], lt[:, c4 * 128:(c4 + 1) * 128])
                            self.cp(blast2[:, dr, tq * 4:(tq + 1) * 4], uu[dr][:, tk].rearrange("p (c t) -> p c t", t=128)[:, :, 127], eng="dve")
                            if dr == 1:
                                self.tt(uu[1][:, tk], lt[:], uu[1][:, tk], ALU.subtract)
                    for dr in range(2):
                        mid = 63 if dr == 0 else 64
                        self.act(dec2[:, dr, :], blast2[:, dr, :], AF.Exp)
                        self.cp(umid2[:, dr, :], uu[dr][:].rearrange("p (c t) -> p c t", t=128)[:, :, mid], eng="dve")
                    S.flush()
                with contextlib.ExitStack() as e2:
                    Wt = self.sb(e2, "Wt", [128, 1024], F32)
                    Et = [self.sb(e2, "Et", [128, 1024], F32) for _ in range(2)]
                    kdec = self.sb(e2, "kdec", [128, S_LEN], BF16)
                    qtl = [self.sb(e2, "qtl", [128, S_LEN], BF16) for _ in range(2)]
                    ktl = [self.sb(e2, "ktl", [128, S_LEN], BF16) for _ in range(2)]
                    qdc = [self.sb(e2, "qdc", [128, S_LEN], BF16) for _ in range(2)]
                    Sbf = [self.sb(e2, "Sbf", [128, 32, 128], BF16) for _ in range(2)]
                    Sf = [self.sb(e2, "Sf", [128, 128], F32) for _ in range(2)]
                    kdt = [self.sb(e2, "kdt", [128, 4, 128], BF16) for _ in range(2)]
                    att = [self.sb(e2, "att", [128, 2, 2, 128], BF16) for _ in range(2)]
                    ss2 = [self.sb(e2, "ss2", [128, 2], F32) for _ in range(3)]
                    rs2 = [self.sb(e2, "rs2", [128, 2], F32) for _ in range(3)]
                    junk = self.sb(e2, "junk", [128, 128], BF16)
                    on = [self.sb(e2, "on", [128, 2, 128], BF16) for _ in range(3)]
                    og = [self.sb(e2, "ogg", [128, 2, 512], BF16) for _ in range(2)]
                    for dr in range(2):
                        bb = uu[dr]
                        for hf in range(4):
                            hs = slice(hf * 1024, (hf + 1) * 1024)
                            cs8 = slice(hf * 8, (hf + 1) * 8)
                            u3 = bb[:, hs].rearrange("p (c t) -> p c t", t=128)
                            W3 = Wt[:].rearrange("p (c t) -> p c t", t=128)
                            um_b = umid2[:, dr, cs8].unsqueeze(2).broadcast_to([128, 8, 128])
                            bl_b = blast2[:, dr, cs8].unsqueeze(2).broadcast_to([128, 8, 128])
                            self.tt(W3, u3, um_b, ALU.subtract)
                            self.act(Et[0][:], Wt[:], AF.Exp)
                            self.tt(qtl[dr][:, hs], q_sb[:, hs], Et[0][:], ALU.mult)
                            self.act(Et[1][:], Wt[:], AF.Exp, scale=-1.0)
                            self.tt(ktl[dr][:, hs], k_sb[:, hs], Et[1][:], ALU.mult)
                            if dr == 0:
                                self.act(Et[0][:], bb[:, hs], AF.Exp)
                                self.tt(qdc[dr][:, hs], q_sb[:, hs], Et[0][:], ALU.mult)
                                self.tt(W3, u3, bl_b, ALU.subtract)
                                self.act(Et[1][:], Wt[:], AF.Exp, scale=-1.0)
                                self.tt(kdec[:, hs], k_sb[:, hs], Et[1][:], ALU.mult)
                            else:
                                self.tt(W3, u3, bl_b, ALU.add)
                                self.act(Et[0][:], Wt[:], AF.Exp)
                                self.tt(qdc[dr][:, hs], q_sb[:, hs], Et[0][:], ALU.mult)
                                self.act(Et[1][:], bb[:, hs], AF.Exp, scale=-1.0)
                                self.tt(kdec[:, hs], k_sb[:, hs], Et[1][:], ALU.mult)
                        self.memset(Sf[0][:], 0.0)
                        n_ = 0
                        groups = list(range(8)) if dr == 0 else list(range(7, -1, -1))
                        for gi_, gq in enumerate(groups):
                            chunks = [gq * 4 + j for j in range(4)]
                            if dr == 1:
                                chunks = chunks[::-1]
                            pT = self.bank_bf(2 + gi_ % 2)
                            kt_ = kdt[gi_ % 2]
                            for j, c in enumerate(chunks):
                                self.tr(pT[:, j * 128:(j + 1) * 128], kdec[:, c * 128:(c + 1) * 128])
                            self.cp(kt_[:], pT[:, 0:512].rearrange("p (j t) -> p j t", j=4), eng="act")
                            pkv = self.bank(4 + gi_ % 2)
                            for j, c in enumerate(chunks):
                                for e_ in range(2):
                                    rows = slice(e_ * 64, (e_ + 1) * 64)
                                    self.mm(pkv[rows, j * 128:(j + 1) * 128], kt_[:, j, rows], v_sb[:, c, e_ * 128:(e_ + 1) * 128], True, True)
                            for j, c in enumerate(chunks):
                                cur, nxt = Sf[n_ % 2], Sf[(n_ + 1) % 2]
                                n_ += 1
                                self.cp(Sbf[dr][:, c, :], cur[:], eng="act")
                                self.stt(nxt[:], cur[:], dec2[:, dr, c:c + 1], pkv[:, j * 128:(j + 1) * 128], ALU.mult, ALU.add)
                    units = []
                    for c in range(32):
                        def p1(c=c):
                            cs = slice(c * 128, (c + 1) * 128)
                            pA2 = [self.bank(2 * (c % 2) + e_) for e_ in range(2)]
                            for dr in range(2):
                                for e_ in range(2):
                                    rows = slice(e_ * 64, (e_ + 1) * 64)
                                    self.mm(pA2[e_][:, dr * 128:(dr + 1) * 128], ktl[dr][rows, cs], qtl[dr][rows, cs], True, True)

                        def p2(c=c):
                            pA2 = [self.bank(2 * (c % 2) + e_) for e_ in range(2)]
                            for e_ in range(2):
                                self.tt(att[c % 2][:, e_, :, :], pA2[e_][:, 0:256].rearrange("p (d n) -> p d n", d=2), gmask[:], ALU.mult)

                        def p3(c=c):
                            cs = slice(c * 128, (c + 1) * 128)
                            pO = self.bank(4 + c % 2)
                            for e_ in range(2):
                                rows = slice(e_ * 64, (e_ + 1) * 64)
                                oo = pO[:, e_ * 128:(e_ + 1) * 128]
                                vv = v_sb[:, c, e_ * 128:(e_ + 1) * 128]
                                self.mm(oo, att[c % 2][:, e_, 0, :], vv, True, False)
                                self.mm(oo, att[c % 2][:, e_, 1, :], vv, False, False)
                                self.mm(oo, qdc[0][rows, cs], Sbf[0][rows, c, :], False, False)
                                self.mm(oo, qdc[1][rows, cs], Sbf[1][rows, c, :], False, True)

                        def p4(c=c):
                            i = c % 3
                            pO = self.bank(4 + c % 2)
                            for e_ in range(2):
                                self.act(junk[:], pO[:, e_ * 128:(e_ + 1) * 128], AF.Square, scale=float(128 ** -0.5), accum_out=ss2[i][:, e_:e_ + 1])
                            self.rstd(rs2[i][:], ss2[i][:])
                            for e_ in range(2):
                                self.act(on[i][:, e_, :], pO[:, e_ * 128:(e_ + 1) * 128], AF.Copy, scale=rs2[i][:, e_:e_ + 1])

                        def p5(c=c):
                            i = c % 3
                            cs = slice(c * 128, (c + 1) * 128)
                            pT = self.bank_bf(6 + c % 2)
                            for e_ in range(2):
                                self.tr(pT[:, e_ * 128:(e_ + 1) * 128], on[i][:, e_, :])
                            ob = og[(c // 4) % 2]
                            for e_ in range(2):
                                self.stt(ob[:, e_, (c % 4) * 128:(c % 4 + 1) * 128], pT[:, e_ * 128:(e_ + 1) * 128], gg[:, 2 * hp + e_: 2 * hp + e_ + 1],
                                         z_sb[:, e_, cs], ALU.mult, ALU.mult)
                            if c % 4 == 3:
                                tk = slice((c // 4) * 512, (c // 4 + 1) * 512)
                                self.dma(self.oT_d[:, 10 + 2 * hp: 12 + 2 * hp, tk], ob[:])
                        units.append((p1, p2, p3, p4, p5))
                    offs = (0, 0, 1, 1, 2)
                    for t in range(32 + 2):
                        for p_, off in enumerate(offs):
                            k = t - off
                            if 0 <= k < 32:
                                units[k][p_]()
                    S.flush()

    def stage_final(self, l, x_src, x_dst, last):
        S = self.S
        I = self.I
        with contextlib.ExitStack() as es:
            wm = self.sb(es, "wm", [128, 8, 4096], BF16)
            wp = self.sb(es, "wp", [128, 14, D], BF16)
            wo = self.sb(es, "wo", [128, 8, D], BF16)
            bm = self.sb(es, "bm", [128, 32], F32)
            gb = self.sb(es, "gbf", [128, D], F32)
            hts = [self.sb(es, "ht", [128, 8, 512], BF16) for _ in range(2)]
            otl = [self.sb(es, "otl", [128, 14, 512], BF16) for _ in range(2)]
            gate = [self.sb(es, "gate", [128, 512], F32) for _ in range(2)]
            accs = self.sb(es, "accs", [128, 8, 512], F32)
            tmp = self.sb(es, "tmp", [128, 512], F32)
            mixT = self.sb(es, "mixT", [128, 8, 512], BF16)
            xt = [self.sb(es, "xtf", [128, D], F32) for _ in range(4)]
            sq = self.sb(es, "sqf", [128, D], BF16)
            ss = [self.sb(es, "ssf", [128, 1], F32) for _ in range(4)]
            rs = [self.sb(es, "rsf", [128, 1], F32) for _ in range(4)]
            self.dma(bm[:], I["b_merge"][l])
            for b in range(4):
                self.wload(wm[:, :, b * 1024:(b + 1) * 1024], I["w_merge"][l][:, b * 1024:(b + 1) * 1024])
                c0, c1 = (0, 4, 6, 10)[b], (4, 6, 10, 14)[b]
                self.wload(wp[:, c0:c1, :], I["w_p"][l][c0 * 128:c1 * 128, :])
            self.wload(wo[:], I["w_out"][l])
            if last:
                self.dma(gb[:], I["final_g"].partition_broadcast(128))
            NB = (4, 2, 4, 4)
            CB = (0, 4, 6, 10)
            cnt = 0
            xc = 0
            for tq in range(8):
                tk = slice(tq * 512, (tq + 1) * 512)
                i = tq % 2
                ht = hts[i]
                self.dma(ht[:], self.hT_d[:, :, tk])
                self.dma(otl[i][:], self.oT_d[:, :, tk])
                for s_ in range(4):
                    tt_ = tq * 4 + s_
                    self.dma(xt[s_][:], x_src[tt_ * 128:(tt_ + 1) * 128, :])
                for b in range(4):
                    for oc in range(8):
                        j = cnt % 2
                        cnt += 1
                        pG = self.bank(j)
                        pY = self.bank(2 + j)
                        col = b * 1024 + oc * 128
                        for kc in range(8):
                            self.mm(pG, wm[:, kc, col:col + 128], ht[:, kc, :], kc == 0, kc == 7)
                        self.act(gate[j][:], pG, AF.Sigmoid, bias=bm[:, b * 8 + oc: b * 8 + oc + 1])
                        for kk in range(NB[b]):
                            self.mm(pY, wp[:, CB[b] + kk, oc * 128:(oc + 1) * 128], otl[i][:, CB[b] + kk, :], kk == 0, kk == NB[b] - 1)
                        if b == 0:
                            self.tt(accs[:, oc, :], pY, gate[j][:], ALU.mult)
                        elif b < 3:
                            self.tt(tmp[:], pY, gate[j][:], ALU.mult)
                            self.tt(accs[:, oc, :], accs[:, oc, :], tmp[:], ALU.add)
                        else:
                            self.tt(tmp[:], pY, gate[j][:], ALU.mult)
                            self.tt(mixT[:, oc, :], accs[:, oc, :], tmp[:], ALU.add)
                for s in range(4):
                    tt_ = tq * 4 + s
                    tok = slice(tt_ * 128, (tt_ + 1) * 128)
                    xi = s
                    xc += 1
                    for half in range(2):
                        pX = self.bank(4 + half + 2 * (xi % 2))
                        hs = slice(half * 512, (half + 1) * 512)
                        for kc in range(8):
                            self.mm(pX, mixT[:, kc, s * 128:(s + 1) * 128], wo[:, kc, hs], kc == 0, kc == 7)
                        self.tt(xt[xi][:, hs], pX, xt[xi][:, hs], ALU.add)
                    if last:
                        self.act(sq[:], xt[xi][:], AF.Square, scale=1.0 / 32.0, accum_out=ss[xi][:])
                        self.rstd(rs[xi][:], ss[xi][:])
                        self.stt(xt[xi][:], xt[xi][:], rs[xi][:], gb[:], ALU.mult, ALU.mult)
                    self.dma(x_dst[tok, :], xt[xi][:], q="pool", final=last)
            S.flush()


def _swap_halves(w, dim=64):
    n = w.shape[-1]
    idx = np.arange(n).reshape(-1, 2, dim // 2)[:, ::-1, :].reshape(-1)
    return np.ascontiguousarray(w[..., idx])


def _nat_tables():
    info, ntile = Builder.nat_seq()
    idx_r = np.zeros((ntile, 128, 128), np.int64)
    idx_c = np.zeros((ntile, 128, 128), np.int64)
    mask = np.full((ntile, 128, 128), NEG, np.float32)
    done = set()
    p = np.arange(128)
    for b in range(32):
        base, a_lo, ns = info[b]
        if base in done:
            continue
        done.add(base)
        qr = 2 * b + p // 64
        qc = p % 64
        rs = np.clip(qr - 4, 0, 56)
        ws = np.clip(qc - 8, 0, 48)
        for n_ in range(ns):
            a = a_lo + n_
            kr = 2 * a + p // 64
            kc = p % 64
            valid = ((kr[:, None] >= rs[None, :]) & (kr[:, None] < rs[None, :] + 8)
                     & (kc[:, None] >= ws[None, :]) & (kc[:, None] < ws[None, :] + 16))
            ro = np.clip(kr[:, None] - qr[None, :] + 7, 0, 14)
            co = np.clip(kc[:, None] - qc[None, :] + 15, 0, 30)
            idx_r[base + n_] = ro
            idx_c[base + n_] = co
            mask[base + n_] = np.where(valid, 0.0, NEG)
    return idx_r, idx_c, mask


def prepare(inputs):
    f = lambda a: np.ascontiguousarray(np.asarray(a, dtype=np.float32))
    w_in = f(inputs["w_in"])
    L = w_in.shape[0]
    rot = [_swap_halves(w_in[:, :, 384:448])]
    for g in range(3):
        for j in range(2):
            c0 = OFF["b"] + g * 768 + j * 256
            rot.append(_swap_halves(w_in[:, :, c0:c0 + 256]))
    w_in_rot = np.ascontiguousarray(np.concatenate(rot, axis=-1))
    w_uq = f(inputs["mla_w_uq"])
    w_uq_rot = np.ascontiguousarray(np.concatenate([_swap_halves(w_uq[:, :, h * 192 + 128: h * 192 + 192]) for h in range(4)], axis=-1))
    qg = f(inputs["mla_q_norm_g"]).reshape(L, 2, 128).transpose(0, 2, 1)
    kvg = f(inputs["mla_kv_norm_g"]).reshape(L, 128, 1)
    g3 = np.ascontiguousarray(np.concatenate([qg, kvg], axis=2))
    idx_r, idx_c, nmask = _nat_tables()
    rpb = f(inputs["nat_rpb"])
    nat_g = np.ascontiguousarray(rpb[:, :, idx_r, idx_c])
    inv = np.power(np.float32(10000.0), -np.arange(0, 64, 2, dtype=np.float32) / np.float32(64)).astype(np.float32)
    ang = np.arange(S_LEN, dtype=np.float32)[:, None] * inv[None, :]
    cos = np.cos(ang).astype(np.float32).T
    sin = np.sin(ang).astype(np.float32).T
    ropecos = np.ascontiguousarray(np.concatenate([cos, cos, cos, cos], axis=0))
    ropesin = np.ascontiguousarray(np.concatenate([-sin, sin, -sin, sin], axis=0))
    kk = np.arange(128)[:, None]
    qq = np.arange(128)[None, :]
    dm = np.zeros((128, 3, 128), np.float32)
    for s in range(3):
        ok = np.abs(qq - kk - 128 * (s - 1)) <= 64
        dm[:, s, :] = np.where(ok, 0.0, NEG)
    gm = np.zeros((128, 2, 128), np.float32)
    gm[:, 0, :] = (kk <= qq)
    gm[:, 1, :] = (kk >= qq)
    w_p = np.ascontiguousarray(np.concatenate([f(inputs["w_proj_a"]), f(inputs["w_proj_b"]), f(inputs["w_proj_c"]), f(inputs["w_proj_d"])], axis=1))
    shared = {
        "norm_g": f(inputs["norm_g"]), "final_g": f(inputs["final_norm_g"]), "w_in": w_in, "w_in_rot": w_in_rot,
        "w_uq": w_uq, "w_uq_rot": w_uq_rot, "w_ukv": f(inputs["mla_w_ukv"]), "g3": g3,
        "nat_g": nat_g, "nat_mask": nmask,
        "w_gf": f(inputs["gla_w_gate_f"]), "w_gb": f(inputs["gla_w_gate_b"]),
        "b_gf": np.ascontiguousarray(f(inputs["gla_b_gate_f"]).reshape(L, 2, 128).transpose(0, 2, 1)),
        "b_gb": np.ascontiguousarray(f(inputs["gla_b_gate_b"]).reshape(L, 2, 128).transpose(0, 2, 1)),
        "gla_g": np.ascontiguousarray(f(inputs["gla_norm_g"]).transpose(0, 2, 1)),
        "w_p": w_p, "w_merge": f(inputs["w_merge"]),
        "b_merge": np.ascontiguousarray(f(inputs["b_merge"]).reshape(L, 32, 128).transpose(0, 2, 1)),
        "w_out": f(inputs["w_out"]),
        "ropecos": ropecos, "ropesin": ropesin,
        "dil_mask": np.ascontiguousarray(dm.reshape(128, 384)), "gla_mask": np.ascontiguousarray(gm.reshape(128, 256)),
        "ident": np.eye(128, dtype=np.float32),
        "rotperm": np.ascontiguousarray(np.eye(128, dtype=np.float32)[:, np.arange(128).reshape(2, 2, 32)[:, ::-1, :].reshape(-1)]),
    }
    return shared


_NC_CACHE = {}


def kernel(**inputs):
    x = np.ascontiguousarray(np.asarray(inputs["x"], dtype=np.float32))
    B = x.shape[0]
    shared = prepare(inputs)
    if "nc" not in _NC_CACHE:
        _NC_CACHE["nc"] = Builder().build()
    nc = _NC_CACHE["nc"]
    in_maps = []
    for b in range(B):
        m = dict(shared)
        m["x"] = x[b]
        in_maps.append(m)
    res = run_bass_kernel_spmd(nc, in_maps, core_ids=list(range(B)))
    return np.stack([np.asarray(r["out"], dtype=np.float32) for r in res.results], axis=0)
```

```python
import contextlib
import numpy as np
import concourse.bass as bass
import concourse.mybir as mybir
from concourse.bass_utils import run_bass_kernel_spmd

F32 = mybir.dt.float32
BF16 = mybir.dt.bfloat16
AF = mybir.ActivationFunctionType
ALU = mybir.AluOpType
_DTSZ = {F32: 4, BF16: 2}

N_DMA_SLOTS = 16
EPS = 1e-6
S_LEN = 4096
D = 1024
D_IN = 7136
OFF = dict(a_q=0, a_kv=256, a_kr=384, b=448, c_q=2752, c_k=3264, c_v=3776, d_q=4288, d_k=4544,
           d_v=4800, d_gf=5312, d_gb=5328, z_a=5344, z_b=5856, z_c=6112, z_d=6624)
DIL = ((128, 1), (512, 4), (2048, 16))
NEG = -30000.0


def _region(ap):
    if str(ap.space) == "DRAM":
        return None
    esz = _DTSZ[ap.dtype]
    pairs = ap.ap
    ps, pc = pairs[0]
    off = ap.offset
    p0 = off // ps
    f0 = off % ps
    lo = hi = f0
    for s, c in pairs[1:]:
        ext = (c - 1) * s
        if ext >= 0:
            hi += ext
        else:
            lo += ext
    name = ap.tensor.name
    if name == "PS":
        return (name, 0, 128, (lo * esz) // 2048 * 2048, ((hi + 1) * esz + 2047) // 2048 * 2048)
    return (name, p0, p0 + pc, lo * esz, (hi + 1) * esz)


class _Op:
    __slots__ = ("eng", "fn", "deps", "signal", "tick", "dma", "slot", "rnd", "final", "idx")

    def __init__(self, eng, fn, dma):
        self.eng = eng
        self.fn = fn
        self.deps = set()
        self.signal = False
        self.tick = 0
        self.dma = dma
        self.slot = -1
        self.rnd = 0
        self.final = False


class Sched:
    ENGS = ("pe", "act", "dve", "pool", "sp")
    DMAQ = ("sp", "act", "pool")

    def __init__(self, nc):
        self.nc = nc
        self.ops = []
        self.state = {}
        self.emitted = 0
        self.cnt = {e: 0 for e in self.ENGS}
        self.dcnt = {e: 0 for e in self.DMAQ}
        self.waited = {e: {} for e in self.ENGS}
        self.slot_last = {}
        self.nbar = 0
        self.finals = []
        self.es = None

    def _add(self, op, reads, writes):
        idx = len(self.ops)
        op.idx = idx
        deps = op.deps
        for ap in reads:
            r = _region(ap)
            if r is None:
                continue
            recs = self.state.setdefault(r[0], [])
            found = False
            psum = r[0] == "PS"
            for rec in recs:
                if rec[1] < r[4] and r[3] < rec[2] and rec[3] < r[2] and r[1] < rec[4]:
                    if rec[6] or (psum and rec[7][0] != op.eng):
                        deps.add(rec[5])
                    elif rec[7] == (op.eng, op.dma) and not op.dma and rec[1] == r[3] and rec[2] == r[4] and rec[3] == r[1] and rec[4] == r[2]:
                        rec[5] = idx
                        found = True
            if not found:
                recs.append([r[0], r[3], r[4], r[1], r[2], idx, False, (op.eng, op.dma)])
        for ap in writes:
            w = _region(ap)
            if w is None:
                continue
            recs = self.state.get(w[0], [])
            new = []
            for rec in recs:
                if rec[1] < w[4] and w[3] < rec[2] and rec[3] < w[2] and w[1] < rec[4]:
                    deps.add(rec[5])
                    if rec[1] >= w[3] and rec[2] <= w[4] and rec[3] >= w[1] and rec[4] <= w[2]:
                        continue
                new.append(rec)
            new.append([w[0], w[3], w[4], w[1], w[2], idx, True, (op.eng, op.dma)])
            self.state[w[0]] = new
        deps.discard(idx)
        self.ops.append(op)
        return op

    def op(self, eng, fn, reads=(), writes=()):
        return self._add(_Op(eng, fn, False), reads, writes)

    def dma(self, eng, fn, reads=(), writes=(), final=False):
        o = _Op(eng, fn, True)
        o.final = final
        return self._add(o, reads, writes)

    def begin(self):
        nc = self.nc
        self.es = contextlib.ExitStack()
        es = self.es
        self.esem = {e: es.enter_context(nc.semaphore("s_" + e)) for e in self.ENGS}
        self.dsem = {q: [es.enter_context(nc.semaphore("d_%s%d" % (q, i))) for i in range(N_DMA_SLOTS)] for q in self.DMAQ}
        self.bar = es.enter_context(nc.semaphore("s_bar"))
        self.block = es.enter_context(nc.Block())
        b = self.block
        self.handles = {"pe": b.tensor, "act": b.scalar, "dve": b.vector, "pool": b.gpsimd, "sp": b.sync}

    def flush(self):
        ops = self.ops
        new = ops[self.emitted:]
        if not new:
            return
        for o in new:
            if o.eng == "pe" and not o.dma:
                o.deps = {d for d in o.deps if not (ops[d].eng == "pe" and not ops[d].dma)}
            for d in o.deps:
                ops[d].signal = True
        last = {}
        for o in new:
            if not o.dma:
                last[o.eng] = o
        for o in last.values():
            o.signal = True
        for o in new:
            if o.dma:
                k = self.dcnt[o.eng]
                self.dcnt[o.eng] += 1
                o.slot = k % N_DMA_SLOTS
                o.rnd = k // N_DMA_SLOTS
                self.slot_last[(o.eng, o.slot)] = o.rnd
                if o.final:
                    self.finals.append(o)
            elif o.signal:
                self.cnt[o.eng] += 1
                o.tick = self.cnt[o.eng]
        esem, dsem = self.esem, self.dsem

        def make_body(E):
            waited = self.waited[E]

            def body(e):
                for o in new:
                    if o.eng != E:
                        continue
                    if o.dma and o.rnd > 0:
                        key = ("d", E, o.slot)
                        v = 16 * o.rnd
                        if waited.get(key, 0) < v:
                            e.wait_ge(dsem[E][o.slot], v)
                            waited[key] = v
                    for d in sorted(o.deps):
                        p = ops[d]
                        if p.dma:
                            key = ("d", p.eng, p.slot)
                            v = 16 * (p.rnd + 1)
                            s = dsem[p.eng][p.slot]
                        else:
                            key = ("e", p.eng)
                            v = p.tick
                            s = esem[p.eng]
                        if waited.get(key, 0) < v:
                            e.wait_ge(s, v)
                            waited[key] = v
                    ins = o.fn(e)
                    if o.dma:
                        ins.then_inc(dsem[E][o.slot], 16)
                    elif o.signal:
                        ins.then_inc(esem[E], 1)
                if E == "sp":
                    for E2 in self.ENGS:
                        v = self.cnt[E2]
                        if v > 0 and waited.get(("e", E2), 0) < v:
                            e.wait_ge(esem[E2], v)
                            waited[("e", E2)] = v
                    for (q, slot), rnd in self.slot_last.items():
                        v = 16 * (rnd + 1)
                        if waited.get(("d", q, slot), 0) < v:
                            e.wait_ge(dsem[q][slot], v)
                            waited[("d", q, slot)] = v
                    e.sem_inc(self.bar, 1)
                else:
                    e.wait_ge(self.bar, self.nbar + 1)
            return body

        for E in self.ENGS:
            self.handles[E](make_body(E))
        self.nbar += 1
        for E in self.ENGS:
            w = self.waited[E]
            for E2 in self.ENGS:
                w[("e", E2)] = self.cnt[E2]
            for (q, slot), rnd in self.slot_last.items():
                w[("d", q, slot)] = 16 * (rnd + 1)
        self.emitted = len(ops)
        self.state = {}
        for o in new:
            o.fn = None

    def end(self):
        self.flush()
        self.es.close()


class Builder:
    def __init__(self, n_layers=2, debug=False, stages="NABCDF"):
        self.n_layers = n_layers
        self.debug = debug
        self.stages = stages
        self.nc = bass.Bass("TRN2", target_bir_lowering=False)
        self.S = Sched(self.nc)
        self.names = {}

    @staticmethod
    def _aps(*xs):
        return [x for x in xs if hasattr(x, "ap") and hasattr(x, "tensor")]

    def mm(self, out, lhsT, rhs, start, stop):
        self.S.op("pe", lambda e: e.matmul(out, lhsT=lhsT, rhs=rhs, start=start, stop=stop, skip_group_check=True), [lhsT, rhs], [out])

    def tr(self, out, in_):
        ident = self.identb[:]
        self.S.op("pe", lambda e: e.transpose(out=out, in_=in_, identity=ident), [in_, ident], [out])

    def act(self, out, in_, func, **kw):
        rd = [in_] + self._aps(kw.get("bias"), kw.get("scale"))
        wr = [out] + self._aps(kw.get("accum_out"))
        self.S.op("act", lambda e: e.activation(out=out, in_=in_, func=func, **kw), rd, wr)

    def tt(self, out, in0, in1, op, eng="dve"):
        self.S.op(eng, lambda e: e.tensor_tensor(out=out, in0=in0, in1=in1, op=op), [in0, in1], [out])

    def ts(self, out, in0, s1, op0, s2=None, op1=None, eng="dve"):
        rd = [in0] + self._aps(s1, s2)
        if op1 is None:
            self.S.op(eng, lambda e: e.tensor_scalar(out=out, in0=in0, scalar1=s1, scalar2=None, op0=op0), rd, [out])
        else:
            self.S.op(eng, lambda e: e.tensor_scalar(out=out, in0=in0, scalar1=s1, scalar2=s2, op0=op0, op1=op1), rd, [out])

    def stt(self, out, in0, scalar, in1, op0, op1):
        rd = [in0, in1] + self._aps(scalar)
        self.S.op("dve", lambda e: e.scalar_tensor_tensor(out=out, in0=in0, scalar=scalar, in1=in1, op0=op0, op1=op1), rd, [out])

    def cp(self, out, in_, eng="dve"):
        if eng == "act":
            self.S.op("act", lambda e: e.activation(out=out, in_=in_, func=AF.Copy), [in_], [out])
        else:
            self.S.op(eng, lambda e: e.tensor_copy(out=out, in_=in_), [in_], [out])

    def recip(self, out, in_):
        self.S.op("dve", lambda e: e.reciprocal(out=out, in_=in_), [in_], [out])

    def memset(self, out, val, eng="dve"):
        self.S.op(eng, lambda e: e.memset(out, val), [], [out])

    def scan(self, out, d0, d1):
        self.S.op("dve", lambda e: e.tensor_tensor_scan(out=out, data0=d0, data1=d1, initial=0.0, op0=ALU.mult, op1=ALU.add), [d0, d1], [out])

    def dma(self, out, in_, q="sp", final=False):
        self.S.dma(q, lambda e: e.dma_start(out=out, in_=in_), [in_], [out], final=final)

    def cdma(self, out, in_):
        n = out.shape[-1]
        if n <= 2048:
            self.dma(out, in_, q="pool")
        else:
            for c0 in range(0, n, 2048):
                c1 = min(n, c0 + 2048)
                self.dma(out[..., c0:c1], in_[..., c0:c1], q="pool")

    def wload(self, dst, src2d):
        self.cdma(dst, src2d.rearrange("(c p) n -> p c n", p=128))

    def rstd(self, rs, ss):
        self.ts(rs, ss, EPS, ALU.add)
        self.act(rs, rs, AF.Sqrt)
        self.recip(rs, rs)

    def rstd_act(self, rs, ss, epsb):
        self.act(rs, ss, AF.Ln, bias=epsb)
        self.act(rs, rs, AF.Exp, scale=-0.5)

    def sb(self, es, name, shape, dt):
        n = self.names.get(name, 0)
        self.names[name] = n + 1
        return es.enter_context(self.nc.sbuf_tensor("%s_%d" % (name, n), list(shape), dt))

    def bank(self, b, n=1):
        return self.PS[:, b * 512:(b + n) * 512]

    def bank_bf(self, b, n=1):
        return self.PS[:, b * 512:(b + n) * 512].bitcast(BF16)

    def declare(self):
        nc = self.nc
        L = 2

        def din(name, shape, dt=F32):
            return nc.dram_tensor(name, list(shape), dt, kind="ExternalInput").ap()

        I = {}
        I["x"] = din("x", [S_LEN, D])
        I["norm_g"] = din("norm_g", [L, D])
        I["final_g"] = din("final_g", [D])
        I["w_in"] = din("w_in", [L, D, D_IN])
        I["w_in_rot"] = din("w_in_rot", [L, D, 64 + 1536])
        I["w_uq"] = din("w_uq", [L, 256, 768])
        I["w_uq_rot"] = din("w_uq_rot", [L, 256, 256])
        I["w_ukv"] = din("w_ukv", [L, 128, 1024])
        I["g3"] = din("g3", [L, 128, 3])
        I["nat_g"] = din("nat_g", [L, 8, 21, 128, 128])
        I["nat_mask"] = din("nat_mask", [21, 128, 128])
        I["w_gf"] = din("w_gf", [L, 16, 256])
        I["w_gb"] = din("w_gb", [L, 16, 256])
        I["b_gf"] = din("b_gf", [L, 128, 2])
        I["b_gb"] = din("b_gb", [L, 128, 2])
        I["gla_g"] = din("gla_g", [L, 128, 4])
        I["w_p"] = din("w_p", [L, 1792, D])
        I["w_merge"] = din("w_merge", [L, D, 4 * D])
        I["b_merge"] = din("b_merge", [L, 128, 32])
        I["w_out"] = din("w_out", [L, D, D])
        I["ropecos"] = din("ropecos", [128, S_LEN])
        I["ropesin"] = din("ropesin", [128, S_LEN])
        I["dil_mask"] = din("dil_mask", [128, 3 * 128])
        I["gla_mask"] = din("gla_mask", [128, 2 * 128])
        I["ident"] = din("ident", [128, 128])
        I["rotperm"] = din("rotperm", [128, 128])
        self.I = I
        kind_dbg = "ExternalOutput" if self.debug else "Internal"
        self.out = nc.dram_tensor("out", [S_LEN, D], F32, kind="ExternalOutput").ap()
        self.hT_d = nc.dram_tensor("hT_d", [128, 8, S_LEN], BF16, kind=kind_dbg).ap()
        self.oT_d = nc.dram_tensor("oT_d", [128, 14, S_LEN], BF16, kind=kind_dbg).ap()
        self.x1_d = nc.dram_tensor("x1_d", [S_LEN, D], F32, kind=kind_dbg).ap()

    def build(self):
        nc = self.nc
        self.declare()
        S = self.S
        with contextlib.ExitStack() as gs:
            self.PS = gs.enter_context(nc.psum_tensor("PS", [128, 4096], F32))
            self.identb = self.sb(gs, "identb", [128, 128], BF16)
            self.onesb = self.sb(gs, "onesb", [128, 128], BF16)
            S.begin()
            self.cdma(self.identb[:], self.I["ident"])
            self.memset(self.onesb[:], 1.0)
            S.flush()
            for l in range(self.n_layers):
                last = l == self.n_layers - 1
                x_src = self.I["x"] if l == 0 else self.x1_d
                x_dst = self.out if last else self.x1_d
                with contextlib.ExitStack() as ls:
                    self.hT = self.sb(ls, "hT", [128, 8, S_LEN], BF16)
                    if "N" in self.stages:
                        self.stage_norm(l, x_src)
                    if "A" in self.stages:
                        self.stage_mla(l)
                    if "B" in self.stages:
                        self.stage_dil(l)
                    if "C" in self.stages:
                        self.stage_nat(l)
                if "D" in self.stages:
                    self.stage_gla(l)
                if "F" in self.stages:
                    self.stage_final(l, x_src, x_dst, last)
            S.end()
        return nc

    def stage_norm(self, l, x_src):
        hT = self.hT
        with contextlib.ExitStack() as es:
            NB_ = 4
            xt = [self.sb(es, "xt", [128, D], F32) for _ in range(NB_)]
            sq = self.sb(es, "sq", [128, D], BF16)
            gb = self.sb(es, "gb", [128, D], F32)
            ss = [self.sb(es, "ss", [128, 1], F32) for _ in range(NB_)]
            rs = [self.sb(es, "rs", [128, 1], F32) for _ in range(NB_)]
            xn = [self.sb(es, "xn", [128, D], BF16) for _ in range(NB_)]
            epsb = self.sb(es, "epsb", [128, 1], F32)
            self.memset(epsb[:], EPS)
            self.dma(gb[:], self.I["norm_g"][l].partition_broadcast(128))
            units = []
            for tt in range(32):
                def p1(tt=tt):
                    i = tt % NB_
                    tok = slice(tt * 128, (tt + 1) * 128)
                    self.dma(xt[i][:], x_src[tok, :])
                    self.act(sq[:], xt[i][:], AF.Square, scale=1.0 / 32.0, accum_out=ss[i][:])
                    self.rstd_act(rs[i][:], ss[i][:], epsb[:])
                    self.stt(xn[i][:], xt[i][:], rs[i][:], gb[:], ALU.mult, ALU.mult)

                def p3(tt=tt):
                    i = tt % NB_
                    tok = slice(tt * 128, (tt + 1) * 128)
                    pT = self.bank_bf(tt % 4)
                    for c in range(8):
                        self.tr(pT[:, c * 128:(c + 1) * 128], xn[i][:, c * 128:(c + 1) * 128])
                    self.cp(hT[:, :, tok], pT.rearrange("p (c t) -> p c t", c=8), eng=("act" if tt % 2 == 0 else "dve"))
                units.append((p1, lambda: None, p3))
            self.pipeline(units, 2)
            for tq in range(8):
                tk = slice(tq * 512, (tq + 1) * 512)
                self.dma(self.hT_d[:, :, tk], hT[:, :, tk])
            self.S.flush()

    @staticmethod
    def pipeline(units, la):
        n = len(units)
        for k in range(n + la):
            if k < n:
                units[k][0]()
                units[k][1]()
            if k - la >= 0:
                units[k - la][2]()

    def rope_combine(self, dst, psA, psB, cosv, sinv, t1, t2):
        self.tt(t1, psA, cosv, ALU.mult)
        self.tt(t2, psB, sinv, ALU.mult)
        self.tt(dst, t1, t2, ALU.add)

    def stage_mla(self, l):
        S = self.S
        I = self.I
        hT = self.hT
        SCALE = float((128 + 64) ** -0.5)
        with contextlib.ExitStack() as es:
            w_lat = self.sb(es, "w_lat", [128, 8, 384], BF16)
            w_kr = self.sb(es, "w_kr", [128, 8, 64], BF16)
            w_krr = self.sb(es, "w_krr", [128, 8, 64], BF16)
            w_uqb = self.sb(es, "w_uqb", [128, 2, 768], BF16)
            w_uqr = self.sb(es, "w_uqr", [128, 2, 256], BF16)
            w_ukvb = self.sb(es, "w_ukvb", [128, 1024], BF16)
            cosb = self.sb(es, "cosb", [128, S_LEN], BF16)
            sinb = self.sb(es, "sinb", [128, S_LEN], BF16)
            g3 = self.sb(es, "g3", [128, 3], F32)
            latT = self.sb(es, "latT", [128, 3, S_LEN], BF16)
            kpeT = self.sb(es, "kpeT", [128, S_LEN], BF16)
            self.wload(w_lat[:], I["w_in"][l][:, 0:384])
            self.wload(w_kr[:], I["w_in"][l][:, 384:448])
            self.wload(w_krr[:], I["w_in_rot"][l][:, 0:64])
            self.wload(w_uqb[:], I["w_uq"][l])
            self.wload(w_uqr[:], I["w_uq_rot"][l])
            self.cdma(w_ukvb[:], I["w_ukv"][l])
            self.cdma(cosb[:], I["ropecos"])
            self.cdma(sinb[:], I["ropesin"])
            self.dma(g3[:], I["g3"][l])
            for es1 in (es,):
                ss2 = [self.sb(es1, "ss2", [128, 2], F32) for _ in range(3)]
                rs2 = [self.sb(es1, "rs2", [128, 2], F32) for _ in range(3)]
                junk = self.sb(es1, "junk", [128, 256], BF16)
                latn = [self.sb(es1, "latn", [128, 384], BF16) for _ in range(3)]
                units = []
                for tt in range(32):
                    def p1(tt=tt):
                        tok = slice(tt * 128, (tt + 1) * 128)
                        psL = self.bank(tt % 3)
                        for kc in range(8):
                            self.mm(psL[:, 0:384], hT[:, kc, tok], w_lat[:, kc, :], kc == 0, kc == 7)

                    def p2(tt=tt):
                        i = tt % 3
                        psL = self.bank(tt % 3)
                        self.act(junk[:, 0:256], psL[:, 0:256], AF.Square, scale=1.0 / 16.0, accum_out=ss2[i][:, 0:1])
                        self.act(junk[:, 0:128], psL[:, 256:384], AF.Square, scale=float(128 ** -0.5), accum_out=ss2[i][:, 1:2])
                        self.rstd(rs2[i][:], ss2[i][:])
                        self.ts(latn[i][:, 0:256], psL[:, 0:256], rs2[i][:, 0:1], ALU.mult)
                        self.ts(latn[i][:, 256:384], psL[:, 256:384], rs2[i][:, 1:2], ALU.mult)

                    def p3(tt=tt):
                        i = tt % 3
                        tok = slice(tt * 128, (tt + 1) * 128)
                        pT = self.bank_bf(4 + tt % 2)
                        for c in range(3):
                            self.tr(pT[:, c * 128:(c + 1) * 128], latn[i][:, c * 128:(c + 1) * 128])
                        for c in range(3):
                            self.ts(latT[:, c, tok], pT[:, c * 128:(c + 1) * 128], g3[:, c:c + 1], ALU.mult)
                    units.append((p1, p2, p3))
                self.pipeline(units, 2)
            for es2 in (es,):
                t1 = [self.sb(es2, "t1", [128, 512], F32) for _ in range(2)]
                t2 = [self.sb(es2, "t2", [128, 512], F32) for _ in range(2)]
                self.memset(kpeT[64:128, :], 0.0)
                for tq in range(8):
                    i = tq % 2
                    tk = slice(tq * 512, (tq + 1) * 512)
                    pa = self.bank(2 * i)
                    pb = self.bank(2 * i + 1)
                    for kc in range(8):
                        self.mm(pa[0:64, :], w_kr[:, kc, :], hT[:, kc, tk], kc == 0, kc == 7)
                    for kc in range(8):
                        self.mm(pb[0:64, :], w_krr[:, kc, :], hT[:, kc, tk], kc == 0, kc == 7)
                    self.rope_combine(kpeT[0:64, tk], pa[0:64, :], pb[0:64, :], cosb[0:64, tk], sinb[0:64, tk], t1[i][0:64, :], t2[i][0:64, :])
            for es3 in (es,):
                qnT = self.sb(es3, "qnT", [128, S_LEN], BF16)
                qpT = self.sb(es3, "qpT", [128, S_LEN], BF16)
                knT = self.sb(es3, "knT", [128, S_LEN], BF16)
                vh = self.sb(es3, "vh", [128, 32, 128], BF16)
                w_za = self.sb(es3, "w_za", [128, 8, 128], BF16)
                t1 = self.sb(es3, "t1", [128, 512], F32)
                t2 = self.sb(es3, "t2", [128, 512], F32)
                pt = [self.sb(es3, "pt", [128, 512], BF16) for _ in range(8)]
                acc = [self.sb(es3, "acc", [128, 512], F32) for _ in range(2)]
                onesf = self.sb(es3, "onesf", [128, 128], F32)
                psum2 = [self.sb(es3, "psum2", [128, 512], BF16) for _ in range(2)]
                self.memset(onesf[:], 1.0)
                zg = [self.sb(es3, "zg", [128, 512], BF16) for _ in range(2)]
                rec = self.sb(es3, "rec", [128, 512], F32)
                ot = self.sb(es3, "ot", [128, 512], F32)
                og = [self.sb(es3, "og", [128, 512], BF16) for _ in range(2)]
                self.memset(qpT[64:128, :], 0.0)
                for h in range(4):
                    self.wload(w_za[:], I["w_in"][l][:, OFF["z_a"] + h * 128: OFF["z_a"] + (h + 1) * 128])
                    for tq in range(8):
                        tk = slice(tq * 512, (tq + 1) * 512)
                        i = tq % 2
                        pp = self.bank(i)
                        for kc in range(2):
                            self.mm(pp, w_uqb[:, kc, h * 192: h * 192 + 128], latT[:, kc, tk], kc == 0, kc == 1)
                        self.cp(qnT[:, tk], pp, eng="act")
                        pa, pb = self.bank(2), self.bank(3)
                        for kc in range(2):
                            self.mm(pa[0:64, :], w_uqb[:, kc, h * 192 + 128: h * 192 + 192], latT[:, kc, tk], kc == 0, kc == 1)
                        for kc in range(2):
                            self.mm(pb[0:64, :], w_uqr[:, kc, h * 64:(h + 1) * 64], latT[:, kc, tk], kc == 0, kc == 1)
                        self.rope_combine(qpT[0:64, tk], pa[0:64, :], pb[0:64, :], cosb[0:64, tk], sinb[0:64, tk], t1[0:64, :], t2[0:64, :])
                    for tq in range(8):
                        tk = slice(tq * 512, (tq + 1) * 512)
                        i = tq % 2
                        pp = self.bank(i)
                        self.mm(pp, w_ukvb[:, h * 256: h * 256 + 128], latT[:, 2, tk], True, True)
                        self.cp(knT[:, tk], pp, eng="act")
                    for t4 in range(8):
                        i = t4 % 2
                        pp = self.bank(i)
                        for j in range(4):
                            tt = t4 * 4 + j
                            self.mm(pp[:, j * 128:(j + 1) * 128], latT[:, 2, tt * 128:(tt + 1) * 128], w_ukvb[:, h * 256 + 128: h * 256 + 256], True, True)
                        self.cp(vh[:, t4 * 4:(t4 + 1) * 4, :], pp.rearrange("p (j d) -> p j d", j=4), eng="dve")
                    units = []
                    for tq in range(8):
                        for kt in range(32):
                            cnt = tq * 32 + kt

                            def p1(tq=tq, kt=kt, cnt=cnt):
                                tk = slice(tq * 512, (tq + 1) * 512)
                                if kt == 0:
                                    pz = self.bank(7)
                                    for kc in range(8):
                                        self.mm(pz, w_za[:, kc, :], hT[:, kc, tk], kc == 0, kc == 7)
                                    self.act(zg[tq % 2][:], pz, AF.Silu)
                                ks = slice(kt * 128, (kt + 1) * 128)
                                pS = self.bank(cnt % 4)
                                self.mm(pS, knT[:, ks], qnT[:, tk], True, False)
                                self.mm(pS, kpeT[:, ks], qpT[:, tk], False, True)

                            def p2(cnt=cnt):
                                self.act(pt[cnt % 8][:], self.bank(cnt % 4), AF.Exp, scale=SCALE)

                            def p3(tq=tq, kt=kt, cnt=cnt):
                                tk = slice(tq * 512, (tq + 1) * 512)
                                o = tq % 2
                                pO = self.bank(4 + o)
                                self.mm(pO, vh[:, kt, :], pt[cnt % 8][:], kt == 0, kt == 31)
                                if kt % 2 == 1:
                                    pa_, pb_ = pt[(cnt - 1) % 8], pt[cnt % 8]
                                    if kt == 1:
                                        self.tt(acc[o][:], pa_[:], pb_[:], ALU.add)
                                    else:
                                        s01 = psum2[(cnt // 2) % 2]
                                        self.tt(s01[:], pa_[:], pb_[:], ALU.add)
                                        self.tt(acc[o][:], acc[o][:], s01[:], ALU.add)
                                if kt == 31:
                                    pD = self.bank(6)
                                    self.mm(pD, onesf[:], acc[o][:], True, True)
                                    self.act(rec[:], pD, AF.Ln)
                                    self.act(rec[:], rec[:], AF.Exp, scale=-1.0)
                                    self.tt(ot[:], pO, rec[:], ALU.mult)
                                    self.tt(og[o][:], ot[:], zg[o][:], ALU.mult)
                                    self.dma(self.oT_d[:, h, tk], og[o][:])
                            units.append((p1, p2, p3))
                    self.pipeline(units, 3)
                S.flush()

    def stage_dil(self, l):
        S = self.S
        I = self.I
        hT = self.hT
        with contextlib.ExitStack() as es:
            cosb = self.sb(es, "cosb", [128, S_LEN], BF16)
            sinb = self.sb(es, "sinb", [128, S_LEN], BF16)
            dmask = self.sb(es, "dmask", [128, 3, 128], BF16)
            numT = self.sb(es, "numT", [128, S_LEN], F32)
            denT = self.sb(es, "denT", [128, S_LEN], F32)
            zT2 = [self.sb(es, "zT", [128, S_LEN], BF16) for _ in range(2)]
            qT = self.sb(es, "qT", [128, S_LEN], BF16)
            kT = self.sb(es, "kT", [128, S_LEN], BF16)
            vg = self.sb(es, "vg", [128, 32, 128], BF16)
            wq = self.sb(es, "wq", [128, 8, 128], BF16)
            wk = self.sb(es, "wk", [128, 8, 128], BF16)
            rotp = self.sb(es, "rotp", [128, 128], BF16)
            xbs = [self.sb(es, "xbs", [128, 512], BF16) for _ in range(2)]
            tmpb = [self.sb(es, "tmpb", [128, 512], BF16) for _ in range(2)]
            wv = self.sb(es, "wv", [128, 8, 128], BF16)
            wz = self.sb(es, "wz", [128, 8, 128], BF16)
            t1 = self.sb(es, "t1", [128, 512], F32)
            t2 = self.sb(es, "t2", [128, 512], F32)
            pt = [self.sb(es, "ptd", [128, 2, 3, 128], BF16) for _ in range(3)]
            og = [self.sb(es, "ogd", [128, 512], BF16) for _ in range(2)]
            self.cdma(cosb[:], I["ropecos"])
            self.cdma(sinb[:], I["ropesin"])
            self.cdma(dmask[:], I["dil_mask"].rearrange("p (s n) -> p s n", s=3))
            self.cdma(rotp[:], I["rotperm"])
            win = I["w_in"][l]
            for hp in range(2):
                zT = zT2[hp]
                zc = OFF["z_b"] + hp * 128
                self.wload(wz[:], win[:, zc:zc + 128])
                for tq in range(8):
                    tk = slice(tq * 512, (tq + 1) * 512)
                    i = tq % 2
                    pz = self.bank(i)
                    for kc in range(8):
                        self.mm(pz, wz[:, kc, :], hT[:, kc, tk], kc == 0, kc == 7)
                    self.act(zT[:, tk], pz, AF.Silu)
                for g, (window, r) in enumerate(DIL):
                    Lg = S_LEN // r
                    nb = Lg // 128
                    cq = OFF["b"] + g * 768 + hp * 128
                    ck = cq + 256
                    cv = cq + 512
                    rq = 64 + g * 512 + hp * 128
                    rk = rq + 256
                    self.wload(wq[:], win[:, cq:cq + 128])
                    self.wload(wk[:], win[:, ck:ck + 128])
                    self.wload(wv[:], win[:, cv:cv + 128])
                    units = []
                    for di_, (dst, wa) in enumerate(((qT, wq), (kT, wk))):
                        for tq in range(8):
                            n_ = di_ * 8 + tq

                            def p1(tq=tq, n_=n_, wa=wa):
                                tk = slice(tq * 512, (tq + 1) * 512)
                                pa = self.bank(0 + 2 * (n_ % 2))
                                for kc in range(8):
                                    self.mm(pa, wa[:, kc, :], hT[:, kc, tk], kc == 0, kc == 7)
                                self.cp(xbs[n_ % 2][:], pa, eng="act")

                            def p3(tq=tq, n_=n_, dst=dst, r=r):
                                tk = slice(tq * 512, (tq + 1) * 512)
                                pa, pb = self.bank(0 + 2 * (n_ % 2)), self.bank(1 + 2 * (n_ % 2))
                                self.mm(pb, rotp[:], xbs[n_ % 2][:], True, True)
                                self.tt(t1[:], pa, cosb[:, tk], ALU.mult)
                                self.tt(t2[:], pb, sinb[:, tk], ALU.mult)
                                if r == 1:
                                    self.tt(dst[:, tk], t1[:], t2[:], ALU.add)
                                else:
                                    tb = tmpb[n_ % 2]
                                    self.tt(tb[:], t1[:], t2[:], ALU.add)
                                    ni = 512 // r
                                    i0 = tq * ni
                                    dview = dst[:].rearrange("p (m i) -> p i m", m=r)[:, i0:i0 + ni, :]
                                    self.cp(dview, tb[:].rearrange("p (i m) -> p i m", m=r), eng="pool")
                            units.append((p1, lambda: None, p3))
                    self.pipeline(units, 1)
                    for t4 in range(8):
                        i = 4 + t4 % 2
                        pv = self.bank(i)
                        for j in range(4):
                            ti = t4 * 4 + j
                            m, c = ti // nb, ti % nb
                            t0 = m + r * 128 * c
                            for kc in range(8):
                                self.mm(pv[:, j * 128:(j + 1) * 128], hT[:, kc, t0: t0 + 127 * r + 1: r], wv[:, kc, :], kc == 0, kc == 7)
                        self.cp(vg[:, t4 * 4:(t4 + 1) * 4, :], pv.rearrange("p (j d) -> p j d", j=4), eng="act")
                    units = []
                    for qd in range(8):
                        for bi in range(4):
                            def geom(qd=qd, bi=bi):
                                idx = qd * 4 + bi
                                m, j = idx // nb, idx % nb
                                slots = [s_ for s_ in range(3) if 0 <= j - 1 + s_ < nb]
                                return idx, m, j, slots

                            def p1(qd=qd, bi=bi, geom=geom):
                                idx, m, j, slots = geom()
                                u = idx % 2
                                s_lo, s_hi = slots[0], slots[-1]
                                pSh = [self.bank(2 * u + hd).rearrange("p (s n) -> p s n", n=128) for hd in range(2)]
                                qs = slice(idx * 128, (idx + 1) * 128)
                                for hd in range(2):
                                    self.mm(pSh[hd][:, s_lo:s_hi + 1, :], self.identb[:], dmask[:, s_lo:s_hi + 1, :], True, False)
                                for s_ in slots:
                                    c = j - 1 + s_
                                    ks = slice((m * nb + c) * 128, (m * nb + c + 1) * 128)
                                    for hd in range(2):
                                        rows = slice(hd * 64, (hd + 1) * 64)
                                        self.mm(pSh[hd][:, s_, :], kT[rows, ks], qT[rows, qs], False, True)

                            def p2(qd=qd, bi=bi, geom=geom):
                                idx, m, j, slots = geom()
                                u = idx % 2
                                s_lo, s_hi = slots[0], slots[-1]
                                for hd in range(2):
                                    pSh = self.bank(2 * u + hd).rearrange("p (s n) -> p s n", n=128)
                                    self.act(pt[idx % 3][:, hd, s_lo:s_hi + 1, :], pSh[:, s_lo:s_hi + 1, :], AF.Exp, scale=0.125)

                            def p3(qd=qd, bi=bi, geom=geom, g=g, r=r, Lg=Lg):
                                idx, m, j, slots = geom()
                                s_lo, s_hi = slots[0], slots[-1]
                                o = qd % 2
                                pO = self.bank(4 + o)
                                pD = self.bank(6 + o)
                                ptb = pt[idx % 3]
                                for hd in range(2):
                                    rows = slice(hd * 64, (hd + 1) * 64)
                                    for s_ in slots:
                                        c = j - 1 + s_
                                        self.mm(pO[rows, bi * 128:(bi + 1) * 128], vg[:, m * nb + c, hd * 64:(hd + 1) * 64], ptb[:, hd, s_, :], s_ == s_lo, s_ == s_hi)
                                    for s_ in slots:
                                        self.mm(pD[rows, bi * 128:(bi + 1) * 128], self.onesb[:, 0:64], ptb[:, hd, s_, :], s_ == s_lo, s_ == s_hi)
                                if bi != 3:
                                    return
                                p0 = qd * 512
                                if r == 1:
                                    nview, dview, po_v, pd_v = numT[:, p0:p0 + 512], denT[:, p0:p0 + 512], pO, pD
                                elif Lg >= 512:
                                    m0, i0_ = p0 // Lg, p0 % Lg
                                    nview = numT[:].rearrange("p (i m) -> p m i", m=r)[:, m0, i0_:i0_ + 512]
                                    dview = denT[:].rearrange("p (i m) -> p m i", m=r)[:, m0, i0_:i0_ + 512]
                                    po_v, pd_v = pO, pD
                                else:
                                    nm_ = 512 // Lg
                                    m0 = p0 // Lg
                                    nview = numT[:].rearrange("p (i m) -> p m i", m=r)[:, m0:m0 + nm_, :]
                                    dview = denT[:].rearrange("p (i m) -> p m i", m=r)[:, m0:m0 + nm_, :]
                                    po_v = pO.rearrange("p (a b) -> p a b", a=nm_)
                                    pd_v = pD.rearrange("p (a b) -> p a b", a=nm_)
                                if g == 0:
                                    self.cp(nview, po_v, eng="dve")
                                    self.cp(dview, pd_v, eng="dve")
                                else:
                                    self.tt(nview, po_v, nview, ALU.add)
                                    self.tt(dview, pd_v, dview, ALU.add)
                            units.append((p1, p2, p3))
                    self.pipeline(units, 1)
                for tq in range(8):
                    tk = slice(tq * 512, (tq + 1) * 512)
                    o = tq % 2
                    self.act(denT[:, tk], denT[:, tk], AF.Ln)
                    self.act(denT[:, tk], denT[:, tk], AF.Exp, scale=-1.0)
                    self.tt(numT[:, tk], numT[:, tk], denT[:, tk], ALU.mult)
                    self.tt(og[o][:], numT[:, tk], zT[:, tk], ALU.mult)
                    self.dma(self.oT_d[:, 4 + hp, tk], og[o][:])
            S.flush()

    @staticmethod
    def nat_seq():
        n = 5
        info = {}
        for b in range(32):
            rs0 = min(max(2 * b - 4, 0), 56)
            rs1 = min(max(2 * b + 1 - 4, 0), 56)
            a_lo, a_hi = rs0 // 2, (rs1 + 7) // 2
            cnt = a_hi - a_lo + 1
            if 2 <= b <= 29:
                assert a_lo == b - 2 and cnt == 5
                info[b] = (0, a_lo, cnt)
            else:
                info[b] = (n, a_lo, cnt)
                n += cnt
        return info, n

    def stage_nat(self, l):
        S = self.S
        I = self.I
        hT = self.hT
        info, ntile = self.nat_seq()
        assert ntile == 21
        with contextlib.ExitStack() as es:
            nmask = self.sb(es, "nmask", [128, 21, 128], BF16)
            nb_sb = self.sb(es, "nb_sb", [128, 2, 21, 128], BF16)
            zT = self.sb(es, "zT", [128, S_LEN], BF16)
            qT = self.sb(es, "qTz", [128, 2, S_LEN], BF16)
            kT = self.sb(es, "kT", [128, S_LEN], BF16)
            vt = self.sb(es, "vt", [128, 32, 128], BF16)
            self.memset(qT[64:128, 0, :], 0.0)
            self.memset(qT[0:64, 1, :], 0.0)
            wq = self.sb(es, "wq", [128, 8, 128], BF16)
            wk = self.sb(es, "wk", [128, 8, 128], BF16)
            wv = self.sb(es, "wv", [128, 8, 128], BF16)
            wz = self.sb(es, "wz", [128, 8, 128], BF16)
            pt = [self.sb(es, "ptn", [128, 5, 128], BF16) for _ in range(4)]
            rec = self.sb(es, "rec", [128, 512], F32)
            ot = self.sb(es, "ot", [128, 512], F32)
            og = [self.sb(es, "ogn", [128, 512], BF16) for _ in range(2)]
            self.cdma(nmask[:], I["nat_mask"].rearrange("t k q -> k t q"))
            win = I["w_in"][l]
            for hp in range(4):
                for (wt, off) in ((wq, OFF["c_q"]), (wk, OFF["c_k"]), (wv, OFF["c_v"]), (wz, OFF["z_c"])):
                    self.wload(wt[:], win[:, off + hp * 128: off + (hp + 1) * 128])
                for hd in range(2):
                    self.cdma(nb_sb[:, hd, :, :], I["nat_g"][l, 2 * hp + hd].rearrange("t k q -> k t q"))
                    self.tt(nb_sb[:, hd, :, :], nb_sb[:, hd, :, :], nmask[:], ALU.add)
                for tq in range(8):
                    tk = slice(tq * 512, (tq + 1) * 512)
                    for n_, wt in enumerate((wq, wk, wz)):
                        pp = self.bank(n_ + 3 * (tq % 2))
                        for kc in range(8):
                            self.mm(pp, wt[:, kc, :], hT[:, kc, tk], kc == 0, kc == 7)
                        if n_ == 0:
                            self.act(qT[0:64, 0, tk], pp[0:64, :], AF.Copy, scale=0.125)
                            self.act(qT[64:128, 1, tk], pp[64:128, :], AF.Copy, scale=0.125)
                        elif n_ == 1:
                            self.cp(kT[:, tk], pp, eng="dve")
                        else:
                            self.act(zT[:, tk], pp, AF.Silu)
                for t4 in range(8):
                    i = 6 + t4 % 2
                    pv = self.bank(i)
                    for j in range(4):
                        tt_ = t4 * 4 + j
                        for kc in range(8):
                            self.mm(pv[:, j * 128:(j + 1) * 128], hT[:, kc, tt_ * 128:(tt_ + 1) * 128], wv[:, kc, :], kc == 0, kc == 7)
                    self.cp(vt[:, t4 * 4:(t4 + 1) * 4, :], pv.rearrange("p (j d) -> p j d", j=4), eng="dve")
                units = []
                for qd in range(8):
                    for bi in range(4):
                        for hd in range(2):
                            def p1(qd=qd, bi=bi, hd=hd):
                                b = qd * 4 + bi
                                base, a_lo, ns = info[b]
                                qs = slice(b * 128, (b + 1) * 128)
                                rows = slice(hd * 64, (hd + 1) * 64)
                                pS5 = self.bank(2 * hd, 2).rearrange("p (s n) -> p s n", n=128)
                                n0 = min(ns, 4)
                                self.mm(pS5[:, 0:n0, :], self.identb[:], nb_sb[:, hd, base: base + n0, :], True, False)
                                if ns == 5:
                                    self.mm(pS5[:, 4:5, :], self.identb[:], nb_sb[:, hd, base + 4: base + 5, :], True, False)
                                for s_ in range(ns):
                                    a = a_lo + s_
                                    self.mm(pS5[:, s_, :], kT[:, a * 128:(a + 1) * 128], qT[:, hd, qs], False, True)

                            def p2(qd=qd, bi=bi, hd=hd):
                                b = qd * 4 + bi
                                ns = info[b][2]
                                pS5 = self.bank(2 * hd, 2).rearrange("p (s n) -> p s n", n=128)
                                self.act(pt[2 * hd + b % 2][:, 0:ns, :], pS5[:, 0:ns, :], AF.Exp)

                            def p3(qd=qd, bi=bi, hd=hd):
                                b = qd * 4 + bi
                                base, a_lo, ns = info[b]
                                o = qd % 2
                                pO = self.bank(4 + o)
                                pD = self.bank(6 + o)
                                rows = slice(hd * 64, (hd + 1) * 64)
                                ptb = pt[2 * hd + b % 2]
                                for s_ in range(ns):
                                    a = a_lo + s_
                                    self.mm(pO[rows, bi * 128:(bi + 1) * 128], vt[:, a, hd * 64:(hd + 1) * 64], ptb[:, s_, :], s_ == 0, s_ == ns - 1)
                                for s_ in range(ns):
                                    self.mm(pD[rows, bi * 128:(bi + 1) * 128], self.onesb[:, 0:64], ptb[:, s_, :], s_ == 0, s_ == ns - 1)
                                if bi == 3 and hd == 1:
                                    tk = slice(qd * 512, (qd + 1) * 512)
                                    self.act(rec[:], pD, AF.Ln)
                                    self.act(rec[:], rec[:], AF.Exp, scale=-1.0)
                                    self.tt(ot[:], pO, rec[:], ALU.mult)
                                    self.tt(og[o][:], ot[:], zT[:, tk], ALU.mult)
                                    self.dma(self.oT_d[:, 6 + hp, tk], og[o][:])
                            units.append((p1, p2, p3))
                self.pipeline(units, 1)
            S.flush()

    def stage_gla(self, l):
        S = self.S
        I = self.I
        win = I["w_in"][l]
        with contextlib.ExitStack() as es:
            gmask = self.sb(es, "gmask", [128, 2, 128], BF16)
            w_g = self.sb(es, "w_g", [64, 256], BF16)
            nbias = self.sb(es, "nbias", [128, 2, 2], F32)
            gg = self.sb(es, "gg", [128, 4], F32)
            q_sb = self.sb(es, "q_sb", [128, S_LEN], BF16)
            k_sb = self.sb(es, "k_sb", [128, S_LEN], BF16)
            z_sb = self.sb(es, "z_sb", [128, 2, S_LEN], BF16)
            v_sb = self.sb(es, "v_sb", [128, 32, 256], BF16)
            g_sb = self.sb(es, "g_sb", [64, S_LEN], BF16)
            uu = [self.sb(es, "uu", [128, S_LEN], F32) for _ in range(2)]
            blast2 = self.sb(es, "blast2", [128, 2, 32], F32)
            dec2 = self.sb(es, "dec2", [128, 2, 32], F32)
            umid2 = self.sb(es, "umid2", [128, 2, 32], F32)
            self.cdma(gmask[:], I["gla_mask"].rearrange("p (s n) -> p s n", s=2))
            self.cdma(w_g[0:16, :], I["w_gf"][l])
            self.cdma(w_g[32:48, :], I["w_gb"][l])
            self.dma(nbias[:, 0, :], I["b_gf"][l])
            self.dma(nbias[:, 1, :], I["b_gb"][l])
            self.ts(nbias[:], nbias[:], -1.0, ALU.mult)
            self.dma(gg[:], I["gla_g"][l])
            for hp in range(2):
                with contextlib.ExitStack() as e1:
                    wq = self.sb(e1, "wq", [128, 8, 128], BF16)
                    wk = self.sb(e1, "wk", [128, 8, 128], BF16)
                    wv = self.sb(e1, "wv", [128, 8, 256], BF16)
                    wz = self.sb(e1, "wz", [128, 8, 256], BF16)
                    wg = self.sb(e1, "wg", [128, 8, 32], BF16)
                    ht = [self.sb(e1, "ht", [128, 8, 512], BF16) for _ in range(2)]
                    la_t = [self.sb(e1, "la_t", [128, 512], F32) for _ in range(2)]
                    self.wload(wq[:], win[:, OFF["d_q"] + hp * 128: OFF["d_q"] + (hp + 1) * 128])
                    self.wload(wk[:], win[:, OFF["d_k"] + hp * 128: OFF["d_k"] + (hp + 1) * 128])
                    self.wload(wv[:], win[:, OFF["d_v"] + hp * 256: OFF["d_v"] + (hp + 1) * 256])
                    self.wload(wz[:], win[:, OFF["z_d"] + hp * 256: OFF["z_d"] + (hp + 1) * 256])
                    if hp == 0:
                        self.wload(wg[:], win[:, OFF["d_gf"]:OFF["d_gf"] + 32])
                    for tq in range(8):
                        tk = slice(tq * 512, (tq + 1) * 512)
                        hb = ht[tq % 2]
                        self.dma(hb[:], self.hT_d[:, :, tk])

                        def proj(out_ap, lhs_of_kc):
                            for kc in range(8):
                                self.mm(out_ap, lhs_of_kc(kc), hb[:, kc, :], kc == 0, kc == 7)
                        proj(self.bank(0), lambda kc: wq[:, kc, :])
                        self.act(q_sb[:, tk], self.bank(0), AF.Copy, scale=0.125)
                        proj(self.bank(1), lambda kc: wk[:, kc, :])
                        self.cp(k_sb[:, tk], self.bank(1), eng="dve")
                        for e_ in range(2):
                            proj(self.bank(2 + e_), lambda kc: wz[:, kc, e_ * 128:(e_ + 1) * 128])
                            self.act(z_sb[:, e_, tk], self.bank(2 + e_), AF.Silu)
                        if hp == 0:
                            proj(self.bank(4)[0:16, :], lambda kc: wg[:, kc, 0:16])
                            self.cp(g_sb[0:16, tk], self.bank(4)[0:16, :], eng="dve")
                            proj(self.bank(5)[32:48, :], lambda kc: wg[:, kc, 16:32])
                            self.cp(g_sb[32:48, tk], self.bank(5)[32:48, :], eng="dve")
                        for s2 in range(2):
                            pv = self.bank(6 + s2)
                            for j in range(2):
                                sub = s2 * 2 + j
                                for kc in range(8):
                                    self.mm(pv[:, j * 256:(j + 1) * 256], hb[:, kc, sub * 128:(sub + 1) * 128], wv[:, kc, :], kc == 0, kc == 7)
                            self.cp(v_sb[:, tq * 4 + s2 * 2: tq * 4 + s2 * 2 + 2, :], pv.rearrange("p (j d) -> p j d", j=2), eng="act")
                        for dr in range(2):
                            grow = slice(0, 16) if dr == 0 else slice(32, 48)
                            pp = self.bank(4 + dr)
                            lt = la_t[dr]
                            self.mm(pp, w_g[grow, hp * 128:(hp + 1) * 128], g_sb[grow, tk], True, True)
                            self.act(lt[:], pp, AF.Exp, scale=-1.0, bias=nbias[:, dr, hp:hp + 1])
                            self.act(lt[:], lt[:], AF.Ln, bias=1.0)
                            self.ts(lt[:], lt[:], -1.0 / 16.0, ALU.mult)
                            for c4 in range(4):
                                c = tq * 4 + c4
                                self.scan(uu[dr][:, c * 128:(c + 1) * 128], self.onesb[:, :], lt[:, c4 * 128:(c4 + 1) * 128])
                            self.cp(blast2[:, dr, tq * 4:(tq + 1) * 4], uu[dr][:, tk].rearrange("p (c t) -> p c t", t=128)[:, :, 127], eng="dve")
                            if dr == 1:
                                self.tt(uu[1][:, tk], lt[:], uu[1][:, tk], ALU.subtract)
                    for dr in range(2):
                        mid = 63 if dr == 0 else 64
                        self.act(dec2[:, dr, :], blast2[:, dr, :], AF.Exp)
                        self.cp(umid2[:, dr, :], uu[dr][:].rearrange("p (c t) -> p c t", t=128)[:, :, mid], eng="dve")
                    S.flush()
                with contextlib.ExitStack() as e2:
                    Wt = self.sb(e2, "Wt", [128, 1024], F32)
                    Et = [self.sb(e2, "Et", [128, 1024], F32) for _ in range(2)]
                    kdec = self.sb(e2, "kdec", [128, S_LEN], BF16)
                    qtl = [self.sb(e2, "qtl", [128, S_LEN], BF16) for _ in range(2)]
                    ktl = [self.sb(e2, "ktl", [128, S_LEN], BF16) for _ in range(2)]
                    qdc = [self.sb(e2, "qdc", [128, S_LEN], BF16) for _ in range(2)]
                    Sbf = [self.sb(e2, "Sbf", [128, 32, 128], BF16) for _ in range(2)]
                    Sf = [self.sb(e2, "Sf", [128, 128], F32) for _ in range(2)]
                    kdt = [self.sb(e2, "kdt", [128, 4, 128], BF16) for _ in range(2)]
                    att = [self.sb(e2, "att", [128, 2, 2, 128], BF16) for _ in range(2)]
                    ss2 = [self.sb(e2, "ss2", [128, 2], F32) for _ in range(3)]
                    rs2 = [self.sb(e2, "rs2", [128, 2], F32) for _ in range(3)]
                    junk = self.sb(e2, "junk", [128, 128], BF16)
                    on = [self.sb(e2, "on", [128, 2, 128], BF16) for _ in range(3)]
                    og = [self.sb(e2, "ogg", [128, 2, 512], BF16) for _ in range(2)]
                    for dr in range(2):
                        bb = uu[dr]
                        for hf in range(4):
                            hs = slice(hf * 1024, (hf + 1) * 1024)
                            cs8 = slice(hf * 8, (hf + 1) * 8)
                            u3 = bb[:, hs].rearrange("p (c t) -> p c t", t=128)
                            W3 = Wt[:].rearrange("p (c t) -> p c t", t=128)
                            um_b = umid2[:, dr, cs8].unsqueeze(2).broadcast_to([128, 8, 128])
                            bl_b = blast2[:, dr, cs8].unsqueeze(2).broadcast_to([128, 8, 128])
                            self.tt(W3, u3, um_b, ALU.subtract)
                            self.act(Et[0][:], Wt[:], AF.Exp)
                            self.tt(qtl[dr][:, hs], q_sb[:, hs], Et[0][:], ALU.mult)
                            self.act(Et[1][:], Wt[:], AF.Exp, scale=-1.0)
                            self.tt(ktl[dr][:, hs], k_sb[:, hs], Et[1][:], ALU.mult)
                            if dr == 0:
                                self.act(Et[0][:], bb[:, hs], AF.Exp)
                                self.tt(qdc[dr][:, hs], q_sb[:, hs], Et[0][:], ALU.mult)
                                self.tt(W3, u3, bl_b, ALU.subtract)
                                self.act(Et[1][:], Wt[:], AF.Exp, scale=-1.0)
                                self.tt(kdec[:, hs], k_sb[:, hs], Et[1][:], ALU.mult)
                            else:
                                self.tt(W3, u3, bl_b, ALU.add)
                                self.act(Et[0][:], Wt[:], AF.Exp)
                                self.tt(qdc[dr][:, hs], q_sb[:, hs], Et[0][:], ALU.mult)
                                self.act(Et[1][:], bb[:, hs], AF.Exp, scale=-1.0)
                                self.tt(kdec[:, hs], k_sb[:, hs], Et[1][:], ALU.mult)
                        self.memset(Sf[0][:], 0.0)
                        n_ = 0
                        groups = list(range(8)) if dr == 0 else list(range(7, -1, -1))
                        for gi_, gq in enumerate(groups):
                            chunks = [gq * 4 + j for j in range(4)]
                            if dr == 1:
                                chunks = chunks[::-1]
                            pT = self.bank_bf(2 + gi_ % 2)
                            kt_ = kdt[gi_ % 2]
                            for j, c in enumerate(chunks):
                                self.tr(pT[:, j * 128:(j + 1) * 128], kdec[:, c * 128:(c + 1) * 128])
                            self.cp(kt_[:], pT[:, 0:512].rearrange("p (j t) -> p j t", j=4), eng="act")
                            pkv = self.bank(4 + gi_ % 2)
                            for j, c in enumerate(chunks):
                                for e_ in range(2):
                                    rows = slice(e_ * 64, (e_ + 1) * 64)
                                    self.mm(pkv[rows, j * 128:(j + 1) * 128], kt_[:, j, rows], v_sb[:, c, e_ * 128:(e_ + 1) * 128], True, True)
                            for j, c in enumerate(chunks):
                                cur, nxt = Sf[n_ % 2], Sf[(n_ + 1) % 2]
                                n_ += 1
                                self.cp(Sbf[dr][:, c, :], cur[:], eng="act")
                                self.stt(nxt[:], cur[:], dec2[:, dr, c:c + 1], pkv[:, j * 128:(j + 1) * 128], ALU.mult, ALU.add)
                    units = []
                    for c in range(32):
                        def p1(c=c):
                            cs = slice(c * 128, (c + 1) * 128)
                            pA2 = [self.bank(2 * (c % 2) + e_) for e_ in range(2)]
                            for dr in range(2):
                                for e_ in range(2):
                                    rows = slice(e_ * 64, (e_ + 1) * 64)
                                    self.mm(pA2[e_][:, dr * 128:(dr + 1) * 128], ktl[dr][rows, cs], qtl[dr][rows, cs], True, True)

                        def p2(c=c):
                            pA2 = [self.bank(2 * (c % 2) + e_) for e_ in range(2)]
                            for e_ in range(2):
                                self.tt(att[c % 2][:, e_, :, :], pA2[e_][:, 0:256].rearrange("p (d n) -> p d n", d=2), gmask[:], ALU.mult)

                        def p3(c=c):
                            cs = slice(c * 128, (c + 1) * 128)
                            pO = self.bank(4 + c % 2)
                            for e_ in range(2):
                                rows = slice(e_ * 64, (e_ + 1) * 64)
                                oo = pO[:, e_ * 128:(e_ + 1) * 128]
                                vv = v_sb[:, c, e_ * 128:(e_ + 1) * 128]
                                self.mm(oo, att[c % 2][:, e_, 0, :], vv, True, False)
                                self.mm(oo, att[c % 2][:, e_, 1, :], vv, False, False)
                                self.mm(oo, qdc[0][rows, cs], Sbf[0][rows, c, :], False, False)
                                self.mm(oo, qdc[1][rows, cs], Sbf[1][rows, c, :], False, True)

                        def p4(c=c):
                            i = c % 3
                            pO = self.bank(4 + c % 2)
                            for e_ in range(2):
                                self.act(junk[:], pO[:, e_ * 128:(e_ + 1) * 128], AF.Square, scale=float(128 ** -0.5), accum_out=ss2[i][:, e_:e_ + 1])
                            self.rstd(rs2[i][:], ss2[i][:])
                            for e_ in range(2):
                                self.act(on[i][:, e_, :], pO[:, e_ * 128:(e_ + 1) * 128], AF.Copy, scale=rs2[i][:, e_:e_ + 1])

                        def p5(c=c):
                            i = c % 3
                            cs = slice(c * 128, (c + 1) * 128)
                            pT = self.bank_bf(6 + c % 2)
                            for e_ in range(2):
                                self.tr(pT[:, e_ * 128:(e_ + 1) * 128], on[i][:, e_, :])
                            ob = og[(c // 4) % 2]
                            for e_ in range(2):
                                self.stt(ob[:, e_, (c % 4) * 128:(c % 4 + 1) * 128], pT[:, e_ * 128:(e_ + 1) * 128], gg[:, 2 * hp + e_: 2 * hp + e_ + 1],
                                         z_sb[:, e_, cs], ALU.mult, ALU.mult)
                            if c % 4 == 3:
                                tk = slice((c // 4) * 512, (c // 4 + 1) * 512)
                                self.dma(self.oT_d[:, 10 + 2 * hp: 12 + 2 * hp, tk], ob[:])
                        units.append((p1, p2, p3, p4, p5))
                    offs = (0, 0, 1, 1, 3)
                    for t in range(32 + 3):
                        for p_, off in enumerate(offs):
                            k = t - off
                            if 0 <= k < 32:
                                units[k][p_]()
                    S.flush()

    def stage_final(self, l, x_src, x_dst, last):
        S = self.S
        I = self.I
        with contextlib.ExitStack() as es:
            wm = self.sb(es, "wm", [128, 8, 4096], BF16)
            wp = self.sb(es, "wp", [128, 14, D], BF16)
            wo = self.sb(es, "wo", [128, 8, D], BF16)
            bm = self.sb(es, "bm", [128, 32], F32)
            gb = self.sb(es, "gbf", [128, D], F32)
            hts = [self.sb(es, "ht", [128, 8, 512], BF16) for _ in range(2)]
            otl = [self.sb(es, "otl", [128, 14, 512], BF16) for _ in range(2)]
            gate = [self.sb(es, "gate", [128, 512], F32) for _ in range(2)]
            acc = self.sb(es, "acc", [128, 512], F32)
            tmp = self.sb(es, "tmp", [128, 512], F32)
            mixT = self.sb(es, "mixT", [128, 8, 512], BF16)
            xt = [self.sb(es, "xtf", [128, D], F32) for _ in range(4)]
            sq = self.sb(es, "sqf", [128, D], BF16)
            ss = [self.sb(es, "ssf", [128, 1], F32) for _ in range(4)]
            rs = [self.sb(es, "rsf", [128, 1], F32) for _ in range(4)]
            for kc in range(8):
                self.cdma(wm[:, kc, :], I["w_merge"][l][kc * 128:(kc + 1) * 128, :])
            self.wload(wp[:], I["w_p"][l])
            self.wload(wo[:], I["w_out"][l])
            self.dma(bm[:], I["b_merge"][l])
            if last:
                self.dma(gb[:], I["final_g"].partition_broadcast(128))
            NB = (4, 2, 4, 4)
            CB = (0, 4, 6, 10)
            cnt = 0
            xc = 0
            for tq in range(8):
                tk = slice(tq * 512, (tq + 1) * 512)
                i = tq % 2
                ht = hts[i]
                self.dma(ht[:], self.hT_d[:, :, tk])
                self.dma(otl[i][:], self.oT_d[:, :, tk])
                for s_ in range(4):
                    tt_ = tq * 4 + s_
                    self.dma(xt[s_][:], x_src[tt_ * 128:(tt_ + 1) * 128, :])
                for oc in range(8):
                    for b in range(4):
                        j = cnt % 2
                        cnt += 1
                        pG = self.bank(j)
                        pY = self.bank(2 + j)
                        col = b * 1024 + oc * 128
                        for kc in range(8):
                            self.mm(pG, wm[:, kc, col:col + 128], ht[:, kc, :], kc == 0, kc == 7)
                        self.act(gate[j][:], pG, AF.Sigmoid, bias=bm[:, b * 8 + oc: b * 8 + oc + 1])
                        for kk in range(NB[b]):
                            self.mm(pY, wp[:, CB[b] + kk, oc * 128:(oc + 1) * 128], otl[i][:, CB[b] + kk, :], kk == 0, kk == NB[b] - 1)
                        if b == 0:
                            self.tt(acc[:], pY, gate[j][:], ALU.mult)
                        elif b < 3:
                            self.tt(tmp[:], pY, gate[j][:], ALU.mult)
                            self.tt(acc[:], acc[:], tmp[:], ALU.add)
                        else:
                            self.tt(tmp[:], pY, gate[j][:], ALU.mult)
                            self.tt(mixT[:, oc, :], acc[:], tmp[:], ALU.add)
                for s in range(4):
                    tt_ = tq * 4 + s
                    tok = slice(tt_ * 128, (tt_ + 1) * 128)
                    xi = s
                    xc += 1
                    for half in range(2):
                        pX = self.bank(4 + half + 2 * (xi % 2))
                        hs = slice(half * 512, (half + 1) * 512)
                        for kc in range(8):
                            self.mm(pX, mixT[:, kc, s * 128:(s + 1) * 128], wo[:, kc, hs], kc == 0, kc == 7)
                        self.tt(xt[xi][:, hs], pX, xt[xi][:, hs], ALU.add)
                    if last:
                        self.act(sq[:], xt[xi][:], AF.Square, scale=1.0 / 32.0, accum_out=ss[xi][:])
                        self.rstd(rs[xi][:], ss[xi][:])
                        self.stt(xt[xi][:], xt[xi][:], rs[xi][:], gb[:], ALU.mult, ALU.mult)
                    self.dma(x_dst[tok, :], xt[xi][:], q="pool", final=last)
            S.flush()


def _swap_halves(w, dim=64):
    n = w.shape[-1]
    idx = np.arange(n).reshape(-1, 2, dim // 2)[:, ::-1, :].reshape(-1)
    return np.ascontiguousarray(w[..., idx])


def _nat_tables():
    info, ntile = Builder.nat_seq()
    idx_r = np.zeros((ntile, 128, 128), np.int64)
    idx_c = np.zeros((ntile, 128, 128), np.int64)
    mask = np.full((ntile, 128, 128), NEG, np.float32)
    done = set()
    p = np.arange(128)
    for b in range(32):
        base, a_lo, ns = info[b]
        if base in done:
            continue
        done.add(base)
        qr = 2 * b + p // 64
        qc = p % 64
        rs = np.clip(qr - 4, 0, 56)
        ws = np.clip(qc - 8, 0, 48)
        for n_ in range(ns):
            a = a_lo + n_
            kr = 2 * a + p // 64
            kc = p % 64
            valid = ((kr[:, None] >= rs[None, :]) & (kr[:, None] < rs[None, :] + 8)
                     & (kc[:, None] >= ws[None, :]) & (kc[:, None] < ws[None, :] + 16))
            ro = np.clip(kr[:, None] - qr[None, :] + 7, 0, 14)
            co = np.clip(kc[:, None] - qc[None, :] + 15, 0, 30)
            idx_r[base + n_] = ro
            idx_c[base + n_] = co
            mask[base + n_] = np.where(valid, 0.0, NEG)
    return idx_r, idx_c, mask


def prepare(inputs):
    f = lambda a: np.ascontiguousarray(np.asarray(a, dtype=np.float32))
    w_in = f(inputs["w_in"])
    L = w_in.shape[0]
    rot = [_swap_halves(w_in[:, :, 384:448])]
    for g in range(3):
        for j in range(2):
            c0 = OFF["b"] + g * 768 + j * 256
            rot.append(_swap_halves(w_in[:, :, c0:c0 + 256]))
    w_in_rot = np.ascontiguousarray(np.concatenate(rot, axis=-1))
    w_uq = f(inputs["mla_w_uq"])
    w_uq_rot = np.ascontiguousarray(np.concatenate([_swap_halves(w_uq[:, :, h * 192 + 128: h * 192 + 192]) for h in range(4)], axis=-1))
    qg = f(inputs["mla_q_norm_g"]).reshape(L, 2, 128).transpose(0, 2, 1)
    kvg = f(inputs["mla_kv_norm_g"]).reshape(L, 128, 1)
    g3 = np.ascontiguousarray(np.concatenate([qg, kvg], axis=2))
    idx_r, idx_c, nmask = _nat_tables()
    rpb = f(inputs["nat_rpb"])
    nat_g = np.ascontiguousarray(rpb[:, :, idx_r, idx_c])
    inv = np.power(np.float32(10000.0), -np.arange(0, 64, 2, dtype=np.float32) / np.float32(64)).astype(np.float32)
    ang = np.arange(S_LEN, dtype=np.float32)[:, None] * inv[None, :]
    cos = np.cos(ang).astype(np.float32).T
    sin = np.sin(ang).astype(np.float32).T
    ropecos = np.ascontiguousarray(np.concatenate([cos, cos, cos, cos], axis=0))
    ropesin = np.ascontiguousarray(np.concatenate([-sin, sin, -sin, sin], axis=0))
    kk = np.arange(128)[:, None]
    qq = np.arange(128)[None, :]
    dm = np.zeros((128, 3, 128), np.float32)
    for s in range(3):
        ok = np.abs(qq - kk - 128 * (s - 1)) <= 64
        dm[:, s, :] = np.where(ok, 0.0, NEG)
    gm = np.zeros((128, 2, 128), np.float32)
    gm[:, 0, :] = (kk <= qq)
    gm[:, 1, :] = (kk >= qq)
    w_p = np.ascontiguousarray(np.concatenate([f(inputs["w_proj_a"]), f(inputs["w_proj_b"]), f(inputs["w_proj_c"]), f(inputs["w_proj_d"])], axis=1))
    shared = {
        "norm_g": f(inputs["norm_g"]), "final_g": f(inputs["final_norm_g"]), "w_in": w_in, "w_in_rot": w_in_rot,
        "w_uq": w_uq, "w_uq_rot": w_uq_rot, "w_ukv": f(inputs["mla_w_ukv"]), "g3": g3,
        "nat_g": nat_g, "nat_mask": nmask,
        "w_gf": f(inputs["gla_w_gate_f"]), "w_gb": f(inputs["gla_w_gate_b"]),
        "b_gf": np.ascontiguousarray(f(inputs["gla_b_gate_f"]).reshape(L, 2, 128).transpose(0, 2, 1)),
        "b_gb": np.ascontiguousarray(f(inputs["gla_b_gate_b"]).reshape(L, 2, 128).transpose(0, 2, 1)),
        "gla_g": np.ascontiguousarray(f(inputs["gla_norm_g"]).transpose(0, 2, 1)),
        "w_p": w_p, "w_merge": f(inputs["w_merge"]),
        "b_merge": np.ascontiguousarray(f(inputs["b_merge"]).reshape(L, 32, 128).transpose(0, 2, 1)),
        "w_out": f(inputs["w_out"]),
        "ropecos": ropecos, "ropesin": ropesin,
        "dil_mask": np.ascontiguousarray(dm.reshape(128, 384)), "gla_mask": np.ascontiguousarray(gm.reshape(128, 256)),
        "ident": np.eye(128, dtype=np.float32),
        "rotperm": np.ascontiguousarray(np.eye(128, dtype=np.float32)[:, np.arange(128).reshape(2, 2, 32)[:, ::-1, :].reshape(-1)]),
    }
    return shared


_NC_CACHE = {}


def kernel(**inputs):
    x = np.ascontiguousarray(np.asarray(inputs["x"], dtype=np.float32))
    B = x.shape[0]
    shared = prepare(inputs)
    if "nc" not in _NC_CACHE:
        _NC_CACHE["nc"] = Builder().build()
    nc = _NC_CACHE["nc"]
    in_maps = []
    for b in range(B):
        m = dict(shared)
        m["x"] = x[b]
        in_maps.append(m)
    res = run_bass_kernel_spmd(nc, in_maps, core_ids=list(range(B)))
    return np.stack([np.asarray(r["out"], dtype=np.float32) for r in res.results], axis=0)
```
